# Optimizing a Trainium2 kernel written in Bass

```python
import math
import jax
import jax.numpy as jnp
from jax import lax
import numpy as np

D_MODEL = 1024
BATCH = 16
SEQ = 2048
DEPTH = 4

GRID_W = 64
CTX_LEN = 256
EPS = 1e-6
F32 = jnp.float32

A_HEADS = 8
A_KV_HEADS = 2
A_GROUP = A_HEADS // A_KV_HEADS
A_HEAD_DIM = 64
A_WINDOW = 128
A_BLOCK = 128
A_SCALE = A_HEAD_DIM ** -0.5
A_Q_W = A_HEADS * A_HEAD_DIM
A_KV_W = A_KV_HEADS * A_HEAD_DIM
ROPE_THETA = 10000.0
ROPE_AXIS_DIM = A_HEAD_DIM // 2

B_HEADS = 4
B_HEAD_DIM = 128
B_W = B_HEADS * B_HEAD_DIM
B_CONV = 5
B_CHUNK = 64

C_HEADS = 4
C_HEAD_DIM = 128
C_W = C_HEADS * C_HEAD_DIM
C_CHUNK = 64

N_BRANCH = 3
BRANCH_WIDTHS = (A_Q_W, B_W, C_W)
MIX_W = A_Q_W + B_W + C_W
D_FF = 2816
N_MOD = 9

IN_WIDTHS = (A_Q_W, A_KV_W, A_KV_W,
             3 * B_W, B_W, 2 * B_HEADS, 2 * B_HEADS,
             C_W, 2 * C_W, C_W, C_W,
             N_BRANCH * D_MODEL)
IN_W = sum(IN_WIDTHS)

kernel_name = "hybrid_gqa_deltanet_hgrn2_dit_block"


def _split(h, widths, axis=-1):
    idx = [int(i) for i in np.cumsum(widths)[:-1]]
    return jnp.split(h, idx, axis=axis)


def _rmsnorm(x, g):
    xf = x.astype(F32)
    y = xf * lax.rsqrt(jnp.mean(xf * xf, axis=-1, keepdims=True) + EPS)
    return (y * g.astype(F32)).astype(x.dtype)


def _l2norm(x):
    return x * lax.rsqrt(jnp.sum(x * x, axis=-1, keepdims=True) + EPS)


def _modulate(x, shift, scale):
    return x * (1.0 + scale) + shift


def _swiglu(h, w_gu, w_d):
    gate, up = jnp.split(h @ w_gu, 2, axis=-1)
    return (jax.nn.silu(gate) * up) @ w_d


def _to_heads(a, n_heads):
    B, T = a.shape[:2]
    return a.reshape(B, T, n_heads, -1).transpose(0, 2, 1, 3)


def _axial_rope(T):
    rows = T // GRID_W
    row_pos = jnp.repeat(jnp.arange(rows, dtype=F32), GRID_W)
    col_pos = jnp.tile(jnp.arange(GRID_W, dtype=F32), rows)
    inv = ROPE_THETA ** (-jnp.arange(0, ROPE_AXIS_DIM, 2, dtype=F32) / ROPE_AXIS_DIM)
    ang = jnp.concatenate([row_pos[:, None] * inv, col_pos[:, None] * inv], axis=-1)
    return jnp.cos(ang), jnp.sin(ang)


def _apply_rope(x, cos, sin):
    xf = x.astype(F32)
    x1, x2 = jnp.split(xf, 2, axis=-1)
    cs, sn = cos[None, :, None, :], sin[None, :, None, :]
    return jnp.concatenate([x1 * cs - x2 * sn, x1 * sn + x2 * cs], axis=-1).astype(x.dtype)


def _centred_conv(x, w):
    p = w.shape[0] // 2
    return lax.conv_general_dilated(x, w[:, None, :].astype(x.dtype), (1,), [(p, p)],
                                    dimension_numbers=('NWC', 'WIO', 'NWC'),
                                    feature_group_count=x.shape[-1])


def _sink_column(sink, shape):
    return jnp.broadcast_to(sink.astype(F32).reshape(1, A_KV_HEADS, A_GROUP, 1, 1), shape[:-1] + (1,))


def _window_attention(ql, kl, vl, kc, vc, sink):
    B, T = ql.shape[:2]
    L = kc.shape[1]
    nk = A_BLOCK + 2 * A_WINDOW
    qg = ql.reshape(B, T, A_KV_HEADS, A_GROUP, A_HEAD_DIM)
    pad = ((0, 0), (A_WINDOW, A_WINDOW), (0, 0), (0, 0))
    kp, vp = jnp.pad(kl, pad), jnp.pad(vl, pad)

    def block(n):
        start = n * A_BLOCK
        q = lax.dynamic_slice_in_dim(qg, start, A_BLOCK, axis=1)
        k = lax.dynamic_slice_in_dim(kp, start, nk, axis=1)
        v = lax.dynamic_slice_in_dim(vp, start, nk, axis=1)
        qi = start + jnp.arange(A_BLOCK)
        kj = start - A_WINDOW + jnp.arange(nk)
        valid = (jnp.abs(qi[:, None] - kj[None, :]) <= A_WINDOW) & (kj >= 0) & (kj < T)
        s_loc = jnp.einsum('bqkgd,bskd->bkgqs', q, k, preferred_element_type=F32) * A_SCALE
        s_loc = jnp.where(valid, s_loc, -jnp.inf)
        s_ctx = jnp.einsum('bqkgd,bskd->bkgqs', q, kc, preferred_element_type=F32) * A_SCALE
        s = jnp.concatenate([s_loc, s_ctx, _sink_column(sink, s_loc.shape)], axis=-1)
        p = jax.nn.softmax(s, axis=-1).astype(vl.dtype)
        o = (jnp.einsum('bkgqs,bskd->bqkgd', p[..., :nk], v)
             + jnp.einsum('bkgqs,bskd->bqkgd', p[..., nk:nk + L], vc))
        return o.reshape(B, A_BLOCK, A_Q_W)

    o = lax.map(block, jnp.arange(T // A_BLOCK))
    return jnp.moveaxis(o, 0, 1).reshape(B, T, A_Q_W)


def _context_attention(qc, kc, vc, sink):
    B, L = qc.shape[:2]
    qg = qc.reshape(B, L, A_KV_HEADS, A_GROUP, A_HEAD_DIM)
    s = jnp.einsum('bqkgd,bskd->bkgqs', qg, kc, preferred_element_type=F32) * A_SCALE
    p = jax.nn.softmax(jnp.concatenate([s, _sink_column(sink, s.shape)], axis=-1), axis=-1)
    p = p[..., :L].astype(vc.dtype)
    return jnp.einsum('bkgqs,bskd->bqkgd', p, vc).reshape(B, L, A_Q_W)


def _attention_branch(pc, pl, qk_norm, sink, cos, sin, with_ctx_out):
    def heads(q, k, v):
        B, T = q.shape[:2]
        q = _rmsnorm(q.reshape(B, T, A_HEADS, A_HEAD_DIM), qk_norm[0])
        k = _rmsnorm(k.reshape(B, T, A_KV_HEADS, A_HEAD_DIM), qk_norm[1])
        return q, k, v.reshape(B, T, A_KV_HEADS, A_HEAD_DIM)

    qc, kc, vc = heads(*pc)
    ql, kl, vl = heads(*pl)
    ql, kl = _apply_rope(ql, cos, sin), _apply_rope(kl, cos, sin)
    ol = _window_attention(ql, kl, vl, kc, vc, sink)
    oc = _context_attention(qc, kc, vc, sink) if with_ctx_out else None
    return oc, ol


def _gated_delta_chunked(q, k, v, g, beta, S0):
    B, H, T, _ = q.shape
    Dv = v.shape[-1]
    C = B_CHUNK
    N = T // C
    q, k, v = (a.reshape(B, H, N, C, a.shape[-1]) for a in (q, k, v))
    g = jnp.cumsum(g.reshape(B, H, N, C), axis=-1)
    beta = beta.reshape(B, H, N, C)
    causal = jnp.tril(jnp.ones((C, C), bool))
    strict = jnp.tril(jnp.ones((C, C), bool), -1)
    decay = jnp.exp(jnp.where(causal, g[..., :, None] - g[..., None, :], -jnp.inf))
    kb = k * beta[..., None]
    low = jnp.where(strict, jnp.einsum('bhnid,bhnjd->bhnij', kb, k) * decay, 0.0)
    M = low + jnp.eye(C, dtype=low.dtype)
    rhs = jnp.concatenate([v * beta[..., None], kb * jnp.exp(g)[..., None]], axis=-1)
    sol = lax.linalg.triangular_solve(M, rhs, left_side=True, lower=True, unit_diagonal=True)
    u, w = sol[..., :Dv], sol[..., Dv:]
    a_intra = jnp.where(causal, jnp.einsum('bhnid,bhnjd->bhnij', q, k) * decay, 0.0)

    def step(S, xs):
        qn, kn, un, wn, gn, an = xs
        v_new = un - jnp.einsum('bhck,bhkv->bhcv', wn, S)
        o = (jnp.einsum('bhck,bhkv->bhcv', qn * jnp.exp(gn)[..., None], S)
             + jnp.einsum('bhij,bhjv->bhiv', an, v_new))
        g_last = gn[..., -1:]
        S = (S * jnp.exp(g_last)[..., None]
             + jnp.einsum('bhck,bhcv->bhkv', kn * jnp.exp(g_last - gn)[..., None], v_new))
        return S, o

    xs = tuple(jnp.moveaxis(a, 2, 0) for a in (q, k, u, w, g, a_intra))
    S, o = lax.scan(step, S0, xs)
    return jnp.moveaxis(o, 0, 2).reshape(B, H, T, Dv), S


def _gla_chunked(q, k, v, logf, S0):
    B, H, T, _ = q.shape
    Dv = v.shape[-1]
    C = C_CHUNK
    N = T // C
    q, k, v, logf = (a.reshape(B, H, N, C, a.shape[-1]) for a in (q, k, v, logf))
    b = jnp.cumsum(logf, axis=3)
    causal = jnp.tril(jnp.ones((C, C), bool))[:, :, None]

    def step(S, xs):
        qn, kn, vn, bn = xs
        rel = jnp.exp(jnp.where(causal, bn[:, :, :, None, :] - bn[:, :, None, :, :], -jnp.inf))
        att = jnp.einsum('bhijk,bhjk->bhij', rel * qn[:, :, :, None, :], kn)
        o = (jnp.einsum('bhik,bhkv->bhiv', qn * jnp.exp(bn), S)
             + jnp.einsum('bhij,bhjv->bhiv', att, vn))
        b_last = bn[:, :, -1:, :]
        S = (S * jnp.exp(b_last[:, :, 0, :])[..., None]
             + jnp.einsum('bhjk,bhjv->bhkv', kn * jnp.exp(b_last - bn), vn))
        return S, o

    xs = tuple(jnp.moveaxis(a, 2, 0) for a in (q, k, v, b))
    S, o = lax.scan(step, S0, xs)
    return jnp.moveaxis(o, 0, 2).reshape(B, H, T, Dv), S


def _gated_head_norm(o, z, gain, n_heads):
    B, T = z.shape[:2]
    o = _rmsnorm(o.transpose(0, 2, 1, 3), gain)
    o = o * jax.nn.silu(z.astype(F32).reshape(B, T, n_heads, -1))
    return o.reshape(B, T, -1).astype(z.dtype)


def _flip_t(a, d):
    return jnp.flip(a, axis=2) if d == 1 else a


def _deltanet_branch(pc, pl, conv_w, a_log, dt_bias, norm_g, with_ctx_out):
    def prep(qkv, beta_raw, alpha_raw):
        B, T = qkv.shape[:2]
        qkv = jax.nn.silu(_centred_conv(qkv, conv_w)).astype(F32)
        q, k, v = jnp.split(qkv, 3, axis=-1)
        q = _l2norm(_to_heads(q, B_HEADS)) * (B_HEAD_DIM ** -0.5)
        k = _l2norm(_to_heads(k, B_HEADS))
        v = _to_heads(v, B_HEADS)
        dirs = lambda a: a.astype(F32).reshape(B, T, 2, B_HEADS).transpose(2, 0, 3, 1)
        beta = jax.nn.sigmoid(dirs(beta_raw))
        g = (-jnp.exp(a_log.astype(F32))[:, None, :, None]
             * jax.nn.softplus(dirs(alpha_raw) + dt_bias.astype(F32)[:, None, :, None]))
        return q, k, v, g, beta

    qkv_c, z_c, beta_c, alpha_c = pc
    qkv_l, z_l, beta_l, alpha_l = pl
    qc, kc, vc, gc, bc = prep(qkv_c, beta_c, alpha_c)
    ql, kl, vl, gl, bl = prep(qkv_l, beta_l, alpha_l)
    outs_c, outs_l = [], []
    for d in range(2):
        S0 = jnp.zeros(qc.shape[:2] + (B_HEAD_DIM, B_HEAD_DIM), F32)
        oc, S = _gated_delta_chunked(_flip_t(qc, d), _flip_t(kc, d), _flip_t(vc, d),
                                     _flip_t(gc[d], d), _flip_t(bc[d], d), S0)
        ol, _ = _gated_delta_chunked(_flip_t(ql, d), _flip_t(kl, d), _flip_t(vl, d),
                                     _flip_t(gl[d], d), _flip_t(bl[d], d), S)
        outs_c.append(_flip_t(oc, d))
        outs_l.append(_flip_t(ol, d))
    ol = _gated_head_norm(outs_l[0] + outs_l[1], z_l, norm_g, B_HEADS)
    oc = _gated_head_norm(outs_c[0] + outs_c[1], z_c, norm_g, B_HEADS) if with_ctx_out else None
    return oc, ol


def _hgrn2_branch(pc, pl, lb, norm_g, with_ctx_out):
    lb = lb.astype(F32)[:, None, None, :]

    def prep(q, f_raw, i):
        B, T = q.shape[:2]
        q = _to_heads(jax.nn.silu(q.astype(F32)), C_HEADS) * (C_HEAD_DIM ** -0.5)
        v = _to_heads(i.astype(F32), C_HEADS)
        a = f_raw.astype(F32).reshape(B, T, 2, C_W).transpose(2, 0, 1, 3)
        f = lb + (1.0 - lb) * jax.nn.sigmoid(a)
        k = (1.0 - lb) * jax.nn.sigmoid(-a)
        heads2 = lambda t: t.reshape(2, B, T, C_HEADS, C_HEAD_DIM).transpose(0, 1, 3, 2, 4)
        return q, heads2(k), v, heads2(jnp.log(f))

    q_c, f_c, i_c, o_gate_c = pc
    q_l, f_l, i_l, o_gate_l = pl
    qc, kc, vc, lfc = prep(q_c, f_c, i_c)
    ql, kl, vl, lfl = prep(q_l, f_l, i_l)
    outs_c, outs_l = [], []
    for d in range(2):
        S0 = jnp.zeros(qc.shape[:2] + (C_HEAD_DIM, C_HEAD_DIM), F32)
        oc, S = _gla_chunked(_flip_t(qc, d), _flip_t(kc[d], d), _flip_t(vc, d), _flip_t(lfc[d], d), S0)
        ol, _ = _gla_chunked(_flip_t(ql, d), _flip_t(kl[d], d), _flip_t(vl, d), _flip_t(lfl[d], d), S)
        outs_c.append(_flip_t(oc, d))
        outs_l.append(_flip_t(ol, d))
    ol = _gated_head_norm(outs_l[0] + outs_l[1], o_gate_l, norm_g, C_HEADS)
    oc = _gated_head_norm(outs_c[0] + outs_c[1], o_gate_c, norm_g, C_HEADS) if with_ctx_out else None
    return oc, ol


def _merge(outs, gate_raw, w_branch, w_out):
    gates = jnp.split(jax.nn.sigmoid(gate_raw), N_BRANCH, axis=-1)
    ws = _split(w_branch, BRANCH_WIDTHS, axis=0)
    m = gates[0] * (outs[0] @ ws[0]) + gates[1] * (outs[1] @ ws[1]) + gates[2] * (outs[2] @ ws[2])
    return m @ w_out


def _layer(xc, xl, c_silu, cctx_silu, w_ada, b_ada, norm_g, w_ffn_gu, w_ffn_d, w_in, w_branch, w_out,
           a_qk_norm, a_sink, b_conv, b_a_log, b_dt_bias, b_norm, lb, c_norm, cos, sin, last):
    mod_l = jnp.split((c_silu @ w_ada + b_ada)[:, None, :], N_MOD, axis=-1)
    mod_c = jnp.split(cctx_silu @ w_ada + b_ada, N_MOD, axis=-1)

    def ffn(x, mod, j_norm, j_ffn):
        h = _modulate(_rmsnorm(x, norm_g[j_norm]), mod[3 * j_norm], mod[3 * j_norm + 1])
        return x + 0.5 * mod[3 * j_norm + 2] * _swiglu(h, w_ffn_gu[j_ffn], w_ffn_d[j_ffn])

    xl = ffn(xl, mod_l, 0, 0)
    xc = ffn(xc, mod_c, 0, 0)

    hl = _modulate(_rmsnorm(xl, norm_g[1]), mod_l[3], mod_l[4])
    hc = _modulate(_rmsnorm(xc, norm_g[1]), mod_c[3], mod_c[4])
    pl = _split(hl @ w_in, IN_WIDTHS)
    pc = _split(hc @ w_in, IN_WIDTHS)
    with_ctx_out = not last
    a_oc, a_ol = _attention_branch(pc[0:3], pl[0:3], a_qk_norm, a_sink, cos, sin, with_ctx_out)
    b_oc, b_ol = _deltanet_branch(pc[3:7], pl[3:7], b_conv, b_a_log, b_dt_bias, b_norm, with_ctx_out)
    c_oc, c_ol = _hgrn2_branch(pc[7:11], pl[7:11], lb, c_norm, with_ctx_out)
    xl = xl + mod_l[5] * _merge((a_ol, b_ol, c_ol), pl[11], w_branch, w_out)
    xl = ffn(xl, mod_l, 2, 1)
    if with_ctx_out:
        xc = xc + mod_c[5] * _merge((a_oc, b_oc, c_oc), pc[11], w_branch, w_out)
        xc = ffn(xc, mod_c, 2, 1)
    return xc, xl


def setup_inputs(seed: int = 0) -> dict:
    key = jax.random.key(seed)
    ks = jax.random.split(key, 20)
    D = D_MODEL
    nrm = lambda k, shape, s: jax.random.normal(k, shape, F32) * s
    dt = jnp.exp(jax.random.uniform(ks[16], (DEPTH, 2, B_HEADS), F32, math.log(1e-3), math.log(1e-1)))
    return {
        "x": nrm(ks[0], (BATCH, SEQ, D), 1.0),
        "c": nrm(ks[1], (BATCH, D), 1.0),
        "ctx": nrm(ks[2], (BATCH, CTX_LEN, D), 1.0),
        "c_ctx": nrm(ks[3], (D,), 1.0),
        "w_ada": nrm(ks[4], (DEPTH, D, N_MOD * D), 0.5 * D ** -0.5),
        "b_ada": nrm(ks[5], (DEPTH, N_MOD * D), 0.02),
        "norm_g": 1.0 + nrm(ks[6], (DEPTH, 3, D), 0.02),
        "w_ffn_gu": nrm(ks[7], (DEPTH, 2, D, 2 * D_FF), D ** -0.5),
        "w_ffn_d": nrm(ks[8], (DEPTH, 2, D_FF, D), D_FF ** -0.5),
        "w_in": nrm(ks[9], (DEPTH, D, IN_W), D ** -0.5),
        "w_branch": nrm(ks[10], (DEPTH, MIX_W, D), A_Q_W ** -0.5),
        "w_out": nrm(ks[11], (DEPTH, D, D), D ** -0.5),
        "a_qk_norm": 1.0 + nrm(ks[12], (DEPTH, 2, A_HEAD_DIM), 0.02),
        "a_sink": nrm(ks[13], (DEPTH, A_HEADS), 0.5),
        "b_conv": nrm(ks[14], (DEPTH, B_CONV, 3 * B_W), B_CONV ** -0.5),
        "b_a_log": jnp.log(jax.random.uniform(ks[15], (DEPTH, 2, B_HEADS), F32, 1.0, 16.0)),
        "b_dt_bias": dt + jnp.log(-jnp.expm1(-dt)),
        "b_norm": 1.0 + nrm(ks[17], (DEPTH, B_HEAD_DIM), 0.02),
        "c_lb": nrm(ks[18], (DEPTH, 2, C_W), 0.1),
        "c_norm": 1.0 + nrm(ks[19], (DEPTH, C_HEAD_DIM), 0.02),
    }


def reference(x, c, ctx, c_ctx, w_ada, b_ada, norm_g, w_ffn_gu, w_ffn_d, w_in, w_branch, w_out,
              a_qk_norm, a_sink, b_conv, b_a_log, b_dt_bias, b_norm, c_lb, c_norm):
    T = x.shape[1]
    cos, sin = _axial_rope(T)
    p_lb = jax.nn.softmax(c_lb.astype(F32), axis=0)
    lb_all = jnp.cumsum(p_lb, axis=0) - p_lb[0]
    c_silu = jax.nn.silu(c)
    cctx_silu = jax.nn.silu(c_ctx)
    xl, xc = x, ctx
    for l in range(DEPTH):
        xc, xl = _layer(xc, xl, c_silu, cctx_silu, w_ada[l], b_ada[l], norm_g[l], w_ffn_gu[l], w_ffn_d[l],
                        w_in[l], w_branch[l], w_out[l], a_qk_norm[l], a_sink[l], b_conv[l], b_a_log[l],
                        b_dt_bias[l], b_norm[l], lb_all[l], c_norm[l], cos, sin, l == DEPTH - 1)
    return xl
```

```python
import os as _os
import numpy as np
from contextlib import ExitStack
import concourse.bass as bass
import concourse.mybir as mybir
from concourse.bass_utils import run_bass_kernel_spmd

F32 = mybir.dt.float32
BF16 = mybir.dt.bfloat16
AF = mybir.ActivationFunctionType
ALU = mybir.AluOpType
AX = mybir.AxisListType

NCORE = 8
DEPTH = 4
D = 1024
DFF = 2816
NF = DFF // 128
SEQ = 2048
CTX = 256
TS = SEQ + CTX
NSEQ = 2
NT = NSEQ * TS
NTILE = NT // 128
TPS = TS // 128
IN_W = 8464
EPS = 1e-6


class R:
    def __init__(self, ap, key=None):
        self.ap, self.key = ap, key


class W(R):
    pass


class _St:
    __slots__ = ("w", "rd")

    def __init__(self):
        self.w = None
        self.rd = {}


class K:
    ENG = ("pe", "act", "dve", "pool", "sp")
    NRING = 16

    def __init__(self, nc, stack, nepochs=1):
        self.nc = nc
        self.sems = []
        self.epochs = []
        for ep in range(nepochs):
            eng_sem = {}
            for e in self.ENG:
                eng_sem[e] = len(self.sems)
                self.sems.append(stack.enter_context(nc.semaphore("s%d_%s" % (ep, e))))
            ring = []
            for i in range(self.NRING):
                ring.append(len(self.sems))
                self.sems.append(stack.enter_context(nc.semaphore("s%d_dma%d" % (ep, i))))
            self.epochs.append((eng_sem, ring))
        self.epoch = 0
        self.eng_sem, self.ring = self.epochs[0]
        self.cnt = {e: 0 for e in self.ENG}
        self.ring_val = [0] * self.NRING
        self.dma_i = 0
        self.ops = {e: [] for e in self.ENG}
        self.known = {e: {} for e in self.ENG}
        self.state = {}
        self.excl = set()
        self.n_ops = 0
        self.n_waits = 0

    def _states(self, name, key):
        d = self.state.setdefault(name, {None: _St()})
        if key is None:
            return list(d.values())
        keys = key if isinstance(key, (list, tuple)) else [key]
        out = []
        for kk in keys:
            if kk not in d:
                s = _St()
                s.w = d[None].w
                s.rd = dict(d[None].rd)
                d[kk] = s
            out.append(d[kk])
        return out

    def _deps(self, acc):
        deps = {}
        for a in acc:
            name = a.ap.tensor.name
            ex = name in self.excl
            for st in self._states(name, None if ex else a.key):
                if st.w is not None and deps.get(st.w[0], 0) < st.w[1]:
                    deps[st.w[0]] = st.w[1]
                if ex or isinstance(a, W):
                    for s, v in st.rd.items():
                        if deps.get(s, 0) < v:
                            deps[s] = v
        return deps

    def _commit(self, acc, tok):
        s, v = tok
        for a in acc:
            name = a.ap.tensor.name
            ex = name in self.excl
            for st in self._states(name, None if ex else a.key):
                if ex or isinstance(a, W):
                    st.w = tok
                    st.rd = {}
                elif st.rd.get(s, 0) < v:
                    st.rd[s] = v

    def _emit_waits(self, eng, deps):
        kn = self.known[eng]
        for s, v in deps.items():
            if kn.get(s, 0) >= v:
                continue
            kn[s] = v
            self.ops[eng].append(("wait", s, v))
            self.n_waits += 1

    def I(self, eng, meth, *args, **kw):
        acc = [a for a in args if isinstance(a, R)] + [a for a in kw.values() if isinstance(a, R)]
        deps = self._deps(acc)
        if eng == "pe":
            deps.pop(self.eng_sem["pe"], None)
        self._emit_waits(eng, deps)
        pargs = [a.ap if isinstance(a, R) else a for a in args]
        pkw = {k_: (a.ap if isinstance(a, R) else a) for k_, a in kw.items()}
        self.cnt[eng] += 1
        tok = (self.eng_sem[eng], self.cnt[eng])
        self.ops[eng].append(("op", meth, pargs, pkw, self.eng_sem[eng], 1))
        self._commit(acc, tok)
        self.n_ops += 1

    def dma(self, out, in_, eng="sp", okey=None, ikey=None, **kw):
        acc = [W(out, okey), R(in_, ikey)]
        deps = self._deps(acc)
        i = self.dma_i
        self.dma_i += 1
        slot = i % self.NRING
        s = self.ring[slot]
        if self.ring_val[slot] > 0 and deps.get(s, 0) < self.ring_val[slot]:
            deps[s] = self.ring_val[slot]
        self._emit_waits(eng, deps)
        self.ring_val[slot] += 16
        tok = (s, self.ring_val[slot])
        self.ops[eng].append(("op", "dma_start", [], dict(out=out, in_=in_, **kw), s, 16))
        self._commit(acc, tok)
        self.n_ops += 1

    def barrier(self):
        allt = {}
        for e in self.ENG:
            if self.cnt[e] > 0:
                allt[self.eng_sem[e]] = self.cnt[e]
        for slot in range(self.NRING):
            if self.ring_val[slot] > 0:
                allt[self.ring[slot]] = self.ring_val[slot]
        for e in self.ENG:
            self._emit_waits(e, dict(allt))
        self.state = {}

    def flush(self):
        nc = self.nc
        ops = self.ops
        self.ops = {e: [] for e in self.ENG}
        sems = self.sems

        def run(engobj, lst):
            for o in lst:
                if o[0] == "wait":
                    engobj.wait_ge(sems[o[1]], o[2])
                else:
                    _, meth, pargs, pkw, s, inc = o
                    getattr(engobj, meth)(*pargs, **pkw).then_inc(sems[s], inc)

        with nc.Block() as block:
            if ops["sp"]:
                @block.sync
                def _(e):
                    run(e, ops["sp"])
            if ops["pe"]:
                @block.tensor
                def _(e):
                    run(e, ops["pe"])
            if ops["act"]:
                @block.scalar
                def _(e):
                    run(e, ops["act"])
            if ops["dve"]:
                @block.vector
                def _(e):
                    run(e, ops["dve"])
            if ops["pool"]:
                @block.gpsimd
                def _(e):
                    run(e, ops["pool"])

    def end_stage(self):
        self.barrier()
        self.flush()

    def new_epoch(self):
        self.epoch += 1
        self.eng_sem, self.ring = self.epochs[self.epoch]
        self.cnt = {e: 0 for e in self.ENG}
        self.ring_val = [0] * self.NRING
        self.known = {e: {} for e in self.ENG}
        self.state = {}


class Rot:
    def __init__(self, tiles):
        self.tiles, self.i = tiles, 0

    def next(self):
        t = self.tiles[self.i % len(self.tiles)]
        self.i += 1
        return t


class Ctx:
    def __getattr__(self, n):
        lz = self.__dict__.get("_lazy", {})
        if n in lz:
            ap = self.__dict__["_din"](n, lz[n])
            self.__dict__[n] = ap
            self.__dict__["_used"].append(n)
            return ap
        raise AttributeError(n)


_UID = [0]


def _uname(name):
    _UID[0] += 1
    return "%s_%d" % (name, _UID[0])


def _alloc(nc, st):
    def sb(name, shape, dt=F32):
        return st.enter_context(nc.sbuf_tensor(_uname(name), shape, dt))

    def ps(name, shape, dt=F32):
        return st.enter_context(nc.psum_tensor(_uname(name), shape, dt))

    def sbr(name, n, shape, dt=F32):
        return Rot([sb("%s%d" % (name, i), shape, dt) for i in range(n)])

    def psr(name, n, shape, dt=F32):
        return Rot([ps("%s%d" % (name, i), shape, dt) for i in range(n)])
    return sb, ps, sbr, psr


def mod_row(t):
    s, tt = divmod(t, TPS)
    return 2 if tt < 2 else s


def stage_pre(C):
    k, nc = C.k, C.nc
    with ExitStack() as st:
        sb, ps, sbr, psr = _alloc(nc, st)
        for s in range(NSEQ):
            k.dma(C.xres[s * TS:s * TS + CTX, :], C.ctx_in[s], okey=("c", s))
            k.dma(C.xres[s * TS + CTX:(s + 1) * TS, :], C.x_in[s], okey=("x", s))
        k.dma(C.scT[:], C.cvecT)
        k.I("act", "activation", W(C.scT[:]), R(C.scT[:]), AF.Silu)
        k.dma(C.ident[:], C.cst["ident"])
        k.I("dve", "tensor_copy", W(C.identb[:]), R(C.ident[:]))
        k.I("pool", "memset", W(C.ones[:]), 1.0)
        cl = sb("cl", [1, DEPTH, 1024])
        mx = sb("clmx", [1, 1024])
        sm = sb("clsm", [1, 1024])
        lbt = sb("lbt", [1, DEPTH, 1024])
        k.dma(cl[:], C.c_lb)
        k.I("dve", "tensor_tensor", W(mx[:]), R(cl[:, 0, :]), R(cl[:, 1, :]), ALU.max)
        for j in range(2, DEPTH):
            k.I("dve", "tensor_tensor", W(mx[:]), R(mx[:]), R(cl[:, j, :]), ALU.max)
        for j in range(DEPTH):
            k.I("dve", "tensor_tensor", W(cl[:, j, :]), R(cl[:, j, :]), R(mx[:]), ALU.subtract)
        k.I("act", "activation", W(cl[:]), R(cl[:]), AF.Exp)
        k.I("dve", "tensor_tensor", W(sm[:]), R(cl[:, 0, :]), R(cl[:, 1, :]), ALU.add)
        for j in range(2, DEPTH):
            k.I("dve", "tensor_tensor", W(sm[:]), R(sm[:]), R(cl[:, j, :]), ALU.add)
        k.I("dve", "reciprocal", W(sm[:]), R(sm[:]))
        for j in range(DEPTH):
            k.I("dve", "tensor_tensor", W(cl[:, j, :]), R(cl[:, j, :]), R(sm[:]), ALU.mult)
        lbm = sb("lbm", [1, C.L * DEPTH])
        k.dma(lbm[:], C.lbmask)
        for l_ in range(C.L):
            k.I("dve", "tensor_scalar", W(lbt[:, l_, :]), R(cl[:, 0, :]), R(lbm[:, l_ * DEPTH:l_ * DEPTH + 1]), None, ALU.mult)
            for j in range(1, DEPTH):
                k.I("dve", "scalar_tensor_tensor", W(lbt[:, l_, :]), R(cl[:, j, :]),
                    R(lbm[:, l_ * DEPTH + j:l_ * DEPTH + j + 1]), R(lbt[:, l_, :]), ALU.mult, ALU.add)
        k.dma(C.lb_dram, lbt[:, 0:C.L, :], eng="act")
        k.end_stage()


def stage_mod(C, li):
    k, nc = C.k, C.nc
    with ExitStack() as st:
        sb, ps, sbr, psr = _alloc(nc, st)
        wt = sbr("wada", 2, [128, 8, 512])
        pm = psr("pmod", 2, [3, 512])
        msb = sb("msb", [3, 9 * D])
        bsb = sb("bsb", [3, 9 * D])
        gsb = sb("gsb", [3, 3 * D])
        k.dma(bsb[:], C.b_ada[li:li + 1, :].partition_broadcast(3))
        k.dma(gsb[:], C.norm_g[li:li + 1].partition_broadcast(3))
        for j in range(18):
            w = wt.next()
            k.dma(w[:], C.w_ada[li][:, j * 512:(j + 1) * 512].rearrange("(c p) n -> p c n", p=128))
            p = pm.next()
            for c in range(8):
                k.I("pe", "matmul", W(p[:]), R(C.scT[:, c, :]), R(w[:, c, :]), start=(c == 0), stop=(c == 7))
            k.I("dve", "tensor_tensor", W(msb[:, j * 512:(j + 1) * 512], key=j), R(p[:]),
                R(bsb[:, j * 512:(j + 1) * 512]), ALU.add)
        for jn in range(3):
            sl = slice((3 * jn + 1) * D, (3 * jn + 2) * D)
            k.I("dve", "scalar_tensor_tensor", W(msb[:, sl]), R(msb[:, sl]), 1.0, R(gsb[:, jn * D:(jn + 1) * D]), ALU.add, ALU.mult)
        for idx in (2, 8):
            sl = slice(idx * D, (idx + 1) * D)
            k.I("dve", "tensor_scalar", W(msb[:, sl]), R(msb[:, sl]), 0.5, None, ALU.mult)
        k.dma(C.mod_dram[li], msb[:], eng="act")
        k.end_stage()


def stage_norm(C, li, jn):
    k, nc = C.k, C.nc
    with ExitStack() as st:
        sb, ps, sbr, psr = _alloc(nc, st)
        gs = [sb("gs%d" % r, [128, D]) for r in range(3)]
        sh = [sb("sh%d" % r, [128, D]) for r in range(3)]
        for r in range(3):
            k.dma(gs[r][:], C.mod_dram[li][r:r + 1, (3 * jn + 1) * D:(3 * jn + 2) * D].partition_broadcast(128))
            k.dma(sh[r][:], C.mod_dram[li][r:r + 1, (3 * jn) * D:(3 * jn + 1) * D].partition_broadcast(128))
        xt = sbr("nx", 4, [128, D])
        junk = sbr("njunk", 3, [128, D], BF16)
        ssr = sbr("nss", 4, [128, 1])
        h1r = sbr("nh1", 4, [128, D])
        hbr = sbr("nhb", 4, [128, D], BF16)
        ptr = psr("nptr", 4, [128, 8, 128], BF16)
        for t in range(NTILE):
            r = mod_row(t)
            x = xt.next()
            k.dma(x[:], C.xres[t * 128:(t + 1) * 128, :], ikey=t)
            ss = ssr.next()
            jk = junk.next()
            k.I("dve", "memset", W(ss[:]), 0.0)
            k.I("act", "activation", W(jk[:]), R(x[:]), AF.Square, accum_out=W(ss[:]))
            k.I("act", "activation", W(ss[:]), R(ss[:]), AF.Sqrt, bias=R(C.epsc[:]), scale=1.0 / D)
            k.I("dve", "reciprocal", W(ss[:]), R(ss[:]))
            h1 = h1r.next()
            k.I("dve", "scalar_tensor_tensor", W(h1[:]), R(x[:]), R(ss[:, 0:1]), R(gs[r][:]), ALU.mult, ALU.mult)
            hb = hbr.next()
            k.I("pool", "tensor_tensor", W(hb[:]), R(h1[:]), R(sh[r][:]), ALU.add)
            pt = ptr.next()
            for c in range(8):
                k.I("pe", "transpose", W(pt[:, c, :], key=c), R(hb[:, c * 128:(c + 1) * 128]), R(C.identb[:]))
            k.I("act", "copy", W(C.hT[:, :, t * 128:(t + 1) * 128], key=t), R(pt[:]))
        k.end_stage()


def stage_ffn_up(C, wgu):
    k, nc = C.k, C.nc
    with ExitStack() as st:
        sb, ps, sbr, psr = _alloc(nc, st)
        w32 = sbr("fw32", 2, [128, 8, 256])
        wbr = sbr("fwb", 2, [128, 8, 256], BF16)
        pg = psr("fpg", 2, [128, 512])
        pu = psr("fpu", 2, [128, 512])
        sgr = sbr("fsg", 2, [128, 512])
        abr = sbr("fab", 2, [128, NT], BF16)
        for f in range(NF):
            w = w32.next()
            k.dma(w[:, :, 0:128], wgu[:, f * 128:(f + 1) * 128].rearrange("(c p) n -> p c n", p=128))
            k.dma(w[:, :, 128:256], wgu[:, DFF + f * 128:DFF + (f + 1) * 128].rearrange("(c p) n -> p c n", p=128))
            wb = wbr.next()
            k.I("pool", "tensor_copy", W(wb[:]), R(w[:]))
            ab = abr.next()
            for tb in range(NT // 512):
                g = pg.next()
                u = pu.next()
                ts = slice(tb * 512, (tb + 1) * 512)
                for c in range(8):
                    k.I("pe", "matmul", W(g[:]), R(wb[:, c, 0:128]), R(C.hT[:, c, ts]), start=(c == 0), stop=(c == 7))
                for c in range(8):
                    k.I("pe", "matmul", W(u[:]), R(wb[:, c, 128:256]), R(C.hT[:, c, ts]), start=(c == 0), stop=(c == 7))
                sg = sgr.next()
                k.I("act", "activation", W(sg[:]), R(g[:]), AF.Silu)
                k.I("dve", "tensor_tensor", W(ab[:, ts], key=tb), R(sg[:]), R(u[:]), ALU.mult)
            k.dma(C.actT[f], ab[:], eng="act", okey=f)
        k.end_stage()


def stage_ffn_down(C, li, wd, gidx):
    k, nc = C.k, C.nc
    with ExitStack() as st:
        sb, ps, sbr, psr = _alloc(nc, st)
        wdb = sb("dwb", [128, NF, D], BF16)
        w32 = sbr("dw32", 2, [128, D])
        gb = [sb("dgb%d" % r, [128, D]) for r in range(3)]
        for r in range(3):
            k.dma(gb[r][:], C.mod_dram[li][r:r + 1, gidx * D:(gidx + 1) * D].partition_broadcast(128))
        for f2 in range(NF):
            w = w32.next()
            k.dma(w[:], wd[f2 * 128:(f2 + 1) * 128, :])
            k.I("pool" if f2 % 2 else "dve", "tensor_copy", W(wdb[:, f2, :], key=f2), R(w[:]))
        abr = sbr("dab", 2, [128, NF, 256], BF16)
        xt = sbr("dx", 2, [128, D])
        tmr = sbr("dtm", 2, [128, D])
        xor_ = sbr("dxo", 2, [128, D])
        py = psr("dpy", 4, [128, 512])
        for tb in range(NT // 256):
            a = abr.next()
            k.dma(a[:], C.actT[:, :, tb * 256:(tb + 1) * 256].rearrange("f p n -> p f n"))
            for q in range(2):
                t = tb * 2 + q
                r = mod_row(t)
                x = xt.next()
                k.dma(x[:], C.xres[t * 128:(t + 1) * 128, :], ikey=t)
                tm = tmr.next()
                for half in range(2):
                    p = py.next()
                    hs = slice(half * 512, (half + 1) * 512)
                    for f in range(NF):
                        k.I("pe", "matmul", W(p[:]), R(a[:, f, q * 128:(q + 1) * 128]), R(wdb[:, f, hs]),
                            start=(f == 0), stop=(f == NF - 1))
                    k.I("dve", "tensor_tensor", W(tm[:, hs], key=half), R(p[:]), R(gb[r][:, hs]), ALU.mult)
                xo = xor_.next()
                k.I("pool", "tensor_tensor", W(xo[:]), R(x[:]), R(tm[:]), ALU.add)
                k.dma(C.xres[t * 128:(t + 1) * 128, :], xo[:], eng="act", okey=t)
        k.end_stage()


OQ, OK_, OV = 0, 512, 640
OBQ, OBZ, OBB, OBA = 768, 2304, 2816, 2824
OCQ, OCF, OCI, OCG = 2832, 3344, 4368, 4880
OG = 5392
TMW = 1680


def stage_in_proj(C, li):
    k, nc = C.k, C.nc
    win = C.w_in[li]
    with ExitStack() as st:
        sb, ps, sbr, psr = _alloc(nc, st)
        wtm = sb("wtm", [128, 8, TMW], BF16)
        stg = sbr("wtmst", 2, [128, 8, 256])
        segs = [(OV, 128, 0), (OBB, 16, 128)] + [(OCF + i * 256, 256, 144 + i * 256) for i in range(4)] \
            + [(OCI + i * 256, 256, 1168 + i * 256) for i in range(2)]
        for i, (c0, wd_, d0) in enumerate(segs):
            w = stg.next()
            k.dma(w[:, :, 0:wd_], win[:, c0:c0 + wd_].rearrange("(c p) n -> p c n", p=128))
            k.I("pool" if i % 2 else "dve", "tensor_copy", W(wtm[:, :, d0:d0 + wd_], key=i), R(w[:, :, 0:wd_]))
        pA = psr("ipA", 1, [128, 144])
        pB = psr("ipB", 3, [128, 512])
        ot = sbr("iot", 2, [128, TMW])
        for t in range(NTILE):
            o = ot.next()
            tsl = slice(t * 128, (t + 1) * 128)
            groups = [(0, 144, pA.next()), (144, 512, pB.next()), (656, 512, pB.next()), (1168, 512, pB.next())]
            for gi, (c0, n, p) in enumerate(groups):
                for c in range(8):
                    k.I("pe", "matmul", W(p[:, 0:n]), R(C.hT[:, c, tsl]), R(wtm[:, c, c0:c0 + n]),
                        start=(c == 0), stop=(c == 7))
                if gi % 2:
                    k.I("act", "copy", W(o[:, c0:c0 + n], key=gi), R(p[:, 0:n]))
                else:
                    k.I("dve", "tensor_copy", W(o[:, c0:c0 + n], key=gi), R(p[:, 0:n]))
            k.dma(C.tmv[tsl, :], o[:, 0:128], eng="act")
            k.dma(C.tmba[tsl, :], o[:, 128:144], eng="act")
            k.dma(C.tmf[tsl, :], o[:, 144:1168], eng="act")
            k.dma(C.tmi[tsl, :], o[:, 1168:1680], eng="act")
        chunks = []
        for h in range(8):
            chunks.append((OQ + h * 64, 64, C.pq[h], "copy"))
        for h in range(2):
            chunks.append((OK_ + h * 64, 64, C.pk[h], "copy"))
        for c in range(12):
            chunks.append((OBQ + c * 128, 128, C.pbq[c], "copy"))
        for c in range(4):
            chunks.append((OBZ + c * 128, 128, C.pbz[c], "silu"))
        for c in range(4):
            chunks.append((OCQ + c * 128, 128, C.pcq[c], "silu32"))
        for c in range(4):
            chunks.append((OCG + c * 128, 128, C.pcg[c], "silu"))
        for c in range(24):
            chunks.append((OG + c * 128, 128, C.pgate[c], "sigmoid"))
        w32 = sbr("iw32", 2, [128, 8, 128])
        wbr = sbr("iwb", 2, [128, 8, 128], BF16)
        pp = psr("ipp", 4, [128, 512])
        o32 = sbr("io32", 2, [128, NT])
        o16 = sbr("io16", 2, [128, NT], BF16)
        for ci, (c0, m, dst, kind) in enumerate(chunks):
            w = w32.next()
            k.dma(w[:, :, 0:m], win[:, c0:c0 + m].rearrange("(c p) n -> p c n", p=128))
            wb = wbr.next()
            k.I("pool", "tensor_copy", W(wb[:, :, 0:m]), R(w[:, :, 0:m]))
            ob = o32.next() if kind in ("copy", "silu32") else o16.next()
            for tb in range(NT // 512):
                p = pp.next()
                ts = slice(tb * 512, (tb + 1) * 512)
                for c in range(8):
                    k.I("pe", "matmul", W(p[0:m, :]), R(wb[:, c, 0:m]), R(C.hT[:, c, ts]), start=(c == 0), stop=(c == 7))
                if kind == "copy":
                    if tb % 2:
                        k.I("act", "copy", W(ob[0:m, ts], key=tb), R(p[0:m, :]))
                    else:
                        k.I("dve", "tensor_copy", W(ob[0:m, ts], key=tb), R(p[0:m, :]))
                elif kind == "sigmoid":
                    k.I("act", "activation", W(ob[0:m, ts], key=tb), R(p[0:m, :]), AF.Sigmoid)
                else:
                    k.I("act", "activation", W(ob[0:m, ts], key=tb), R(p[0:m, :]), AF.Silu)
            k.dma(dst, ob[0:m, :], eng="act")
        k.end_stage()


def stage_attn(C, li):
    k, nc = C.k, C.nc
    SCALE = 0.125
    with ExitStack() as st:
        sb, ps, sbr, psr = _alloc(nc, st)
        cosF = sb("cosF", [64, TS])
        sinS = sb("sinS", [64, TS])
        perm = sb("perm", [64, 64])
        mlo = sb("mlo", [128, 4, 128], BF16)
        mhi = sb("mhi", [128, 4, 128], BF16)
        m32 = sb("m32", [128, 4, 128])
        gqk = sb("gqk", [64, 2])
        se = sb("sinke", [64, 8])
        sk = sb("sk", [64, 8, 128])
        onesb = sb("onesb", [128, 64], BF16)
        k.dma(cosF[:], C.cst["cosF"])
        k.dma(sinS[:], C.cst["sinS"])
        k.dma(perm[:], C.cst["perm64"])
        k.dma(m32[:], C.cst["maskLo4"].rearrange("p (h n) -> p h n", h=4))
        k.I("dve", "tensor_copy", W(mlo[:]), R(m32[:]))
        k.dma(m32[:], C.cst["maskHi4"].rearrange("p (h n) -> p h n", h=4))
        k.I("dve", "tensor_copy", W(mhi[:]), R(m32[:]))
        k.dma(gqk[:], C.qknT[li])
        k.dma(se[:], C.a_sink[li:li + 1, :].partition_broadcast(64))
        k.I("act", "activation", W(se[:]), R(se[:]), AF.Exp)
        for h in range(8):
            k.I("dve", "tensor_scalar", W(sk[:, h, :], key=h), R(C.ones[0:64, 0:128]), R(se[:, h:h + 1]), None, ALU.mult)
        k.I("dve", "tensor_copy", W(onesb[:]), R(C.ones[:, 0:64]))

        raw = sbr("araw", 2, [64, TS])
        qT = sb("aqT", [64, 8, TS], BF16)
        kT = sb("akT", [64, 2, TS], BF16)
        v32 = sb("av32", [128, TPS, 128])
        vb = sb("avb", [128, TPS, 128], BF16)
        sqr = sbr("asq", 3, [64, 512])
        rsr = sbr("ars", 3, [64, 512])
        knr = sbr("akn", 3, [64, 512])
        t1r = sbr("at1", 3, [64, 512])
        t2r = sbr("at2", 3, [64, 512])
        pss = psr("apss", 1, [64, 512])
        ppm = psr("appm", 1, [64, 512])
        pst = psr("apst", 2, [128, 4, 128])
        po = psr("apo", 2, [64, 4, 128])
        pd = psr("apd", 2, [64, 4, 128])
        Pr = sbr("aP", 3, [128, 4, 128], BF16)
        Pm = sbr("aPm", 2, [128, 4, 128], BF16)
        rdr = sbr("ard", 2, [64, 4, 128])
        obr = sbr("aob", 2, [64, 4, TS], BF16)
        blocks = [(i * 512, 512) for i in range(4)] + [(2048, 256)]

        def prep(src_dram, gcol, dst):
            r = raw.next()
            k.dma(r[:], src_dram)
            for (b0, n) in blocks:
                cs = slice(b0, b0 + n)
                sq = sqr.next()
                k.I("act", "activation", W(sq[:, 0:n]), R(r[:, cs]), AF.Square)
                p1 = pss.next()
                k.I("pe", "matmul", W(p1[:, 0:n]), R(C.ones[0:64, 0:64]), R(sq[:, 0:n]), start=True, stop=True)
                rs = rsr.next()
                k.I("act", "activation", W(rs[:, 0:n]), R(p1[:, 0:n]), AF.Sqrt, bias=R(C.epsc[0:64, :]), scale=1.0 / 64)
                k.I("dve", "reciprocal", W(rs[:, 0:n]), R(rs[:, 0:n]))
                kn = knr.next()
                k.I("dve", "scalar_tensor_tensor", W(kn[:, 0:n]), R(r[:, cs]), R(gcol), R(rs[:, 0:n]), ALU.mult, ALU.mult)
                p2 = ppm.next()
                k.I("pe", "matmul", W(p2[:, 0:n]), R(perm[:]), R(kn[:, 0:n]), start=True, stop=True)
                t1 = t1r.next()
                k.I("pool", "tensor_tensor", W(t1[:, 0:n]), R(kn[:, 0:n]), R(cosF[:, cs]), ALU.mult)
                t2 = t2r.next()
                k.I("dve", "tensor_tensor", W(t2[:, 0:n]), R(p2[:, 0:n]), R(sinS[:, cs]), ALU.mult)
                k.I("pool", "tensor_tensor", W(dst[:, cs]), R(t1[:, 0:n]), R(t2[:, 0:n]), ALU.add)

        for s in range(NSEQ):
            tok = slice(s * TS, (s + 1) * TS)
            for g in range(2):
                prep(C.pk[g][:, tok], gqk[:, 1:2], kT[:, g, :])
            for h in range(8):
                prep(C.pq[h][:, tok], gqk[:, 0:1], qT[:, h, :])
            k.dma(v32[:], C.tmv[tok, :].rearrange("(t p) n -> p t n", p=128))
            k.I("dve", "tensor_copy", W(vb[:]), R(v32[:]))
            for g in range(2):
                ob = obr.next()
                for qt in range(TPS):
                    if qt < 2:
                        kcs = [(0, None), (1, None)]
                    else:
                        kcs = [(0, None), (1, None)]
                        if qt - 1 >= 2:
                            kcs.append((qt - 1, mlo))
                        kcs.append((qt, None))
                        if qt + 1 < TPS:
                            kcs.append((qt + 1, mhi))
                    o_ps = po.next()
                    d_ps = pd.next()
                    qs = slice(qt * 128, (qt + 1) * 128)
                    def score(kt, msk):
                        s_ps = pst.next()
                        k.I("pe", "matmul", W(s_ps[:]), R(kT[:, g, kt * 128:(kt + 1) * 128]), R(qT[:, 4 * g:4 * g + 4, qs]),
                            start=True, stop=True)
                        P = Pr.next()
                        k.I("act", "activation", W(P[:]), R(s_ps[:]), AF.Exp, scale=SCALE)
                        if msk is not None:
                            P2 = Pm.next()
                            k.I("dve", "tensor_tensor", W(P2[:]), R(P[:]), R(msk[:]), ALU.mult)
                            P = P2
                        return P
                    Pcur = score(*kcs[0])
                    for i, (kt, msk) in enumerate(kcs):
                        Pnext = score(*kcs[i + 1]) if i + 1 < len(kcs) else None
                        k.I("pe", "matmul", W(o_ps[:]), R(vb[:, kt, g * 64:(g + 1) * 64]), R(Pcur[:]),
                            start=(i == 0), stop=(i == len(kcs) - 1))
                        k.I("pe", "matmul", W(d_ps[:]), R(onesb[:]), R(Pcur[:]),
                            start=(i == 0), stop=(i == len(kcs) - 1))
                        Pcur = Pnext
                    rd = rdr.next()
                    k.I("dve", "tensor_tensor", W(rd[:]), R(d_ps[:]), R(sk[:, 4 * g:4 * g + 4, :]), ALU.add)
                    k.I("dve", "reciprocal", W(rd[:]), R(rd[:]))
                    k.I("dve", "tensor_tensor", W(ob[:, :, qs], key=qt), R(o_ps[:]), R(rd[:]), ALU.mult)
                k.dma(C.aout[4 * g:4 * g + 4, :, tok].rearrange("h p n -> p h n"), ob[:], eng="act")
        k.end_stage()


class PSlots:
    def __init__(self, nc, st, nbanks, name, k=None):
        self.banks = [st.enter_context(nc.psum_tensor(_uname(name), [128, 4, 128], F32)) for _ in range(nbanks)]
        if k is not None:
            k.excl.update(b.name for b in self.banks)
        self.i = 0

    def next(self):
        n = len(self.banks) * 4
        j = self.i % n
        self.i += 1
        b, q = j % len(self.banks), j // len(self.banks)
        return self.banks[b], q


def scan_orders():
    fwd = list(range(TPS))
    bwd = [1, 0] + list(range(TPS - 1, 1, -1))
    return fwd, bwd


def head_norm_out(C, k, sb_, oacc, gcol, gate_dram, dst_dram, tok0, tmp):
    sqr, pss, rsr, onr, ggr, obf = tmp
    ob = obf.next()
    for (b0, n) in [(i * 512, 512) for i in range(4)] + [(2048, 256)]:
        cs = slice(b0, b0 + n)
        sq = sqr.next()
        k.I("act", "activation", W(sq[:, 0:n]), R(oacc[:, cs]), AF.Square)
        p1 = pss.next()
        k.I("pe", "matmul", W(p1[:, 0:n]), R(C.ones[:]), R(sq[:, 0:n]), start=True, stop=True)
        rs = rsr.next()
        k.I("act", "activation", W(rs[:, 0:n]), R(p1[:, 0:n]), AF.Sqrt, bias=R(C.epsc[:]), scale=1.0 / 128)
        k.I("dve", "reciprocal", W(rs[:, 0:n]), R(rs[:, 0:n]))
        on = onr.next()
        k.I("dve", "scalar_tensor_tensor", W(on[:, 0:n]), R(oacc[:, cs]), R(gcol), R(rs[:, 0:n]), ALU.mult, ALU.mult)
        gg = ggr.next()
        k.dma(gg[:, 0:n], gate_dram[:, tok0 + b0:tok0 + b0 + n])
        k.I("pool", "tensor_tensor", W(ob[:, cs], key=b0), R(on[:, 0:n]), R(gg[:, 0:n]), ALU.mult)
    k.dma(dst_dram[:, tok0:tok0 + TS], ob[:], eng="act")


def load_scan_consts(C, k, sb):
    cs = {}
    for n in ("triA0", "triA1", "triM0", "triM1", "suf0", "suf1"):
        t = sb(n, [128, 128])
        k.dma(t[:], C.cst[n])
        cs[n] = t
    return cs


def stage_gla(C, li):
    k, nc = C.k, C.nc
    QS = 128 ** -0.5
    NSET = 6
    with ExitStack() as st:
        sb, ps, sbr, psr = _alloc(nc, st)
        sc = load_scan_consts(C, k, sb)
        lb = [sb("lb%d" % d, [128, 512]) for d in range(2)]
        oml = [sb("oml%d" % d, [128, 512]) for d in range(2)]
        for d in range(2):
            k.dma(lb[d][:], C.lb_dram[0:1, li, d * 512:(d + 1) * 512].partition_broadcast(128))
            k.I("dve", "tensor_scalar", W(oml[d][:]), R(lb[d][:]), -1.0, 1.0, ALU.mult, ALU.add)
        gn = sb("gcn", [128, 1])
        k.dma(gn[:], C.c_normT[li])
        banks = [st.enter_context(nc.psum_tensor(_uname("gps"), [128, 4, 128], F32)) for _ in range(NSET)]
        k.excl.update(b.name for b in banks)
        pss = psr("gpss", 2, [128, 512])
        ar = sbr("ga", 2, [128, 512])
        sgr = sbr("gsg", 2, [128, 512])
        t1r = sbr("gt1", 2, [128, 512])
        fr = sbr("gf", 2, [128, 512])
        vir = sbr("gvi", 4, [128, 512])
        qfr = sbr("gqf", 4, [128, 4, 128])
        ktr = sbr("gkt", 4, [128, 512])
        lfr = sbr("glf", 4, [128, 512])
        names = ("e1", "e2", "e3", "e4", "qe", "qt", "ktl", "kh", "at0", "at1")
        sets = [{n_: sb("gu%d%s" % (i, n_), [128, 128]) for n_ in names} for i in range(NSET)]
        mku = [sb("gmk%d" % d, [128, 128], mybir.dt.uint32) for d in range(2)]
        for d in range(2):
            k.I("dve", "tensor_scalar", W(mku[d][:]), R(sc["triA%d" % d][:]), 0.5, None, ALU.is_gt)
        for i in range(NSET):
            k.I("pool", "memset", W(sets[i]["at0"][:]), 0.0)
            k.I("pool", "memset", W(sets[i]["at1"][:]), 0.0)
        S = {(d, h): [sb("gS%d_%d_%d" % (d, h, i), [128, 128]) for i in range(2)] for d in range(2) for h in range(4)}
        oacc = [sb("goacc%d" % h, [128, TS]) for h in range(4)]
        tmp = (sbr("hsq", 2, [128, 512]), pss, sbr("hrs", 2, [128, 512]), sbr("hon", 2, [128, 512]),
               sbr("hgg", 2, [128, 512], BF16), sbr("hob", 2, [128, TS], BF16))
        orders = scan_orders()
        for s in range(NSEQ):
            tok0 = s * TS
            for h in range(4):
                k.I("pool", "memset", W(oacc[h][:]), 0.0)
            sidx = {}
            for kk in S:
                k.I("pool", "memset", W(S[kk][0][:]), 0.0)
                sidx[kk] = 0
            done = {}
            shared = {}

            def ready(spec):
                n, d, h = spec
                return n == 0 or done.get((d, h), -1) >= n - 1

            def mark_done(spec):
                n, d, h = spec
                done[(d, h)] = n

            def unit(spec, si):
                n, d, h = spec
                T_ = sets[si]
                bk = banks[si]

                def Pw(q, cs=None):
                    return W(bk[:, q, :] if cs is None else bk[:, q, cs])

                def Pr(q):
                    return W(bk[:, q, :])
                p = orders[d][n]
                rows = slice(tok0 + p * 128, tok0 + (p + 1) * 128)
                pc = slice(p * 128, (p + 1) * 128)
                triA, triM, suf = sc["triA%d" % d], sc["triM%d" % d], sc["suf%d" % d]
                if h == 0:
                    a = ar.next()
                    k.dma(a[:], C.tmf[rows, d * 512:(d + 1) * 512])
                    vi = vir.next()
                    k.dma(vi[:], C.tmi[rows, :])
                    qf = qfr.next()
                    k.dma(qf[:], C.pcq[:, :, rows].rearrange("h p n -> p h n"))
                    k.I("pool", "tensor_scalar", W(qf[:]), R(qf[:]), QS, None, ALU.mult)
                    sg = sgr.next()
                    k.I("act", "activation", W(sg[:]), R(a[:]), AF.Sigmoid)
                    t1 = t1r.next()
                    k.I("dve", "tensor_tensor", W(t1[:]), R(sg[:]), R(oml[d][:]), ALU.mult)
                    f = fr.next()
                    k.I("pool", "tensor_tensor", W(f[:]), R(t1[:]), R(lb[d][:]), ALU.add)
                    kt = ktr.next()
                    k.I("pool", "tensor_tensor", W(kt[:]), R(oml[d][:]), R(t1[:]), ALU.subtract)
                    lf = lfr.next()
                    k.I("act", "activation", W(lf[:]), R(f[:]), AF.Ln)
                    shared[(n, d)] = (vi, qf, kt, lf)
                    yield
                vi, qf, kt, lf = shared[(n, d)]
                hs = slice(h * 128, (h + 1) * 128)
                e1, e2, e3, e4 = T_["e1"], T_["e2"], T_["e3"], T_["e4"]
                qe, qt_, ktl, kh, at = T_["qe"], T_["qt"], T_["ktl"], T_["kh"], T_["at%d" % d]
                k.I("pe", "matmul", Pw(0), R(lf[:, hs]), R(triA[:]), start=True, stop=True)
                k.I("pe", "matmul", Pw(1), R(lf[:, hs]), R(triM[:]), start=True, stop=True)
                k.I("pe", "matmul", Pw(2), R(suf[:]), R(lf[:, hs]), start=True, stop=True)
                k.I("pe", "transpose", Pw(3), R(kt[:, hs]), R(C.ident[:]))
                yield
                k.I("act", "activation", W(e1[:]), Pr(0), AF.Exp)
                k.I("act", "activation", W(e2[:]), Pr(1), AF.Exp)
                k.I("act", "activation", W(e3[:]), Pr(1), AF.Exp, scale=-1.0)
                k.I("act", "activation", W(e4[:]), Pr(2), AF.Exp)
                yield
                k.I("dve", "tensor_tensor", W(qe[:]), R(qf[:, h, :]), R(e1[:]), ALU.mult)
                k.I("pool", "tensor_tensor", W(qt_[:]), R(qf[:, h, :]), R(e2[:]), ALU.mult)
                k.I("dve", "tensor_tensor", W(ktl[:]), Pr(3), R(e3[:]), ALU.mult)
                k.I("pool", "tensor_tensor", W(kh[:]), R(kt[:, hs]), R(e4[:]), ALU.mult)
                yield
                k.I("pe", "matmul", Pw(0), R(ktl[:]), R(qt_[:]), start=True, stop=True)
                yield
                k.I("dve", "copy_predicated", W(at[:]), R(mku[d][:]), Pr(0))
                yield "S"
                Sl = S[(d, h)]
                for c in ((0, 1) if d == 0 else (1, 0)):
                    cs = slice(c * 64, (c + 1) * 64)
                    Sc = Sl[sidx[(d, h)] % 2]
                    Sn = Sl[(sidx[(d, h)] + 1) % 2]
                    sidx[(d, h)] += 1
                    k.I("pe", "matmul", Pw(1, cs), R(Sc[:]), R(qe[:, cs]), start=True, stop=False)
                    k.I("pe", "matmul", Pw(1, cs), R(vi[cs, hs]), R(at[cs, cs]), start=False, stop=True)
                    k.I("pe", "matmul", Pw(2), R(kh[cs, :]), R(vi[cs, hs]), start=True, stop=True)
                    yield
                    lc = c * 64 + (63 if d == 0 else 0)
                    k.I("dve", "scalar_tensor_tensor", W(Sn[:]), R(Sc[:]), R(e1[:, lc:lc + 1]), Pr(2), ALU.mult, ALU.add)
                    yield
                k.I("dve", "tensor_tensor", W(oacc[h][:, pc], p), R(oacc[h][:, pc], p), Pr(1), ALU.add)

            specs = [(n, d, h) for n in range(TPS) for d in range(2) for h in range(4)]
            run_units(specs, unit, NSET, ready, mark_done)
            for h in range(4):
                head_norm_out(C, k, sb, oacc[h], gn[:, 0:1], C.pcg[h], C.cout[h], tok0, tmp)
        k.end_stage()


def run_units(specs, make_gen, nset, ready, mark_done):
    free = list(range(nset))
    active = []
    it = iter(specs)
    exhausted = False
    while True:
        while free and not exhausted:
            spec = next(it, None)
            if spec is None:
                exhausted = True
                break
            si = free.pop(0)
            active.append([make_gen(spec, si), si, spec, False])
        if not active:
            break
        for a in list(active):
            if a[3]:
                if not ready(a[2]):
                    continue
                a[3] = False
            try:
                r = next(a[0])
                if r == "S" and not ready(a[2]):
                    a[3] = True
            except StopIteration:
                mark_done(a[2])
                active.remove(a)
                free.append(a[1])


def stage_delta(C, li):
    k, nc = C.k, C.nc
    QS = 128 ** -0.5
    NSET = int(_os.environ.get("NSET", "6"))
    orders = scan_orders()
    blocks = [(i * 512, 512) for i in range(4)] + [(2048, 256)]
    for s in range(NSEQ):
        tok0 = s * TS
        tok = slice(tok0, tok0 + TS)
        for hp in range(2):
            with ExitStack() as st:
                sb, ps, sbr, psr = _alloc(nc, st)
                qn = [sb("dqn%d" % h, [128, TS]) for h in range(2)]
                kn = [sb("dkn%d" % h, [128, TS]) for h in range(2)]
                vs = [sb("dvs%d" % h, [128, TS]) for h in range(2)]
                beta = sb("dbeta", [128, TPS, 8])
                gg_ = sb("dg", [128, TPS, 8])
                with ExitStack() as st1:
                    sb1, ps1, sbr1, psr1 = _alloc(nc, st1)
                    convw = sb1("convw", [128, 12, 5])
                    k.dma(convw[:], C.convT[li])
                    nala = sb1("nala", [128, 8])
                    dtb = sb1("dtb", [128, 8])
                    k.dma(nala[:], C.b_a_log[li:li + 1, :].partition_broadcast(128))
                    k.dma(dtb[:], C.b_dt_bias[li:li + 1, :].partition_broadcast(128))
                    k.I("act", "activation", W(nala[:]), R(nala[:]), AF.Exp)
                    k.I("dve", "tensor_scalar", W(nala[:]), R(nala[:]), -1.0, None, ALU.mult)
                    ba = sb1("dba", [128, TPS, 16])
                    pss = psr1("dpss", 2, [128, 512])
                    xraw = sbr1("dxraw", 2, [128, TS])
                    acc = sbr1("dacc", 2, [128, TS])
                    ctmp = sbr1("dctmp", 1, [128, TS])
                    sqr = sbr1("dsq", 2, [128, 512])
                    rsr = sbr1("drs", 2, [128, 512])
                    k.dma(ba[:], C.tmba[tok, :].rearrange("(t p) n -> p t n", p=128))
                    k.I("act", "activation", W(beta[:]), R(ba[:, :, 0:8]), AF.Sigmoid)
                    for t in range(TPS):
                        k.I("dve", "tensor_tensor", W(gg_[:, t, :]), R(ba[:, t, 8:16]), R(dtb[:]), ALU.add)
                    k.I("act", "activation", W(gg_[:]), R(gg_[:]), AF.Exp)
                    k.I("act", "activation", W(gg_[:]), R(gg_[:]), AF.Ln, bias=R(C.ones[:, 0:1]), scale=1.0)
                    for t in range(TPS):
                        k.I("dve", "tensor_tensor", W(gg_[:, t, :]), R(gg_[:, t, :]), R(nala[:]), ALU.mult)

                    def conv_chunk(src_dram, cidx, dst, eng2):
                        x = xraw.next()
                        k.dma(x[:], src_dram)
                        a = acc.next()
                        k.I(eng2, "tensor_scalar", W(a[:]), R(x[:]), R(convw[:, cidx, 2:3]), None, ALU.mult)
                        for (sa, sb_) in ((0, CTX), (CTX, TS)):
                            for j in (0, 1, 3, 4):
                                sft = j - 2
                                t0, t1 = max(sa, sa - sft), min(sb_, sb_ - sft)
                                if eng2 == "dve":
                                    k.I("dve", "scalar_tensor_tensor", W(a[:, t0:t1]), R(x[:, t0 + sft:t1 + sft]),
                                        R(convw[:, cidx, j:j + 1]), R(a[:, t0:t1]), ALU.mult, ALU.add)
                                else:
                                    tp = ctmp.next()
                                    k.I("pool", "tensor_scalar", W(tp[:, t0:t1]), R(x[:, t0 + sft:t1 + sft]),
                                        R(convw[:, cidx, j:j + 1]), None, ALU.mult)
                                    k.I("pool", "tensor_tensor", W(a[:, t0:t1]), R(a[:, t0:t1]), R(tp[:, t0:t1]), ALU.add)
                        k.I("act", "activation", W(dst[:]), R(a[:]), AF.Silu)

                    def l2n(t, scale):
                        for (b0, n) in blocks:
                            cs = slice(b0, b0 + n)
                            sq = sqr.next()
                            k.I("act", "activation", W(sq[:, 0:n]), R(t[:, cs]), AF.Square)
                            p1 = pss.next()
                            k.I("pe", "matmul", W(p1[:, 0:n]), R(C.ones[:]), R(sq[:, 0:n]), start=True, stop=True)
                            rs = rsr.next()
                            k.I("act", "activation", W(rs[:, 0:n]), R(p1[:, 0:n]), AF.Sqrt, bias=R(C.epsc[:]), scale=1.0)
                            k.I("dve", "reciprocal", W(rs[:, 0:n]), R(rs[:, 0:n]))
                            k.I("dve", "scalar_tensor_tensor", W(t[:, cs]), R(t[:, cs]), scale, R(rs[:, 0:n]), ALU.mult, ALU.mult)

                    for h in (2 * hp, 2 * hp + 1):
                        conv_chunk(C.pbq[h][:, tok], h, qn[h % 2], "dve")
                        l2n(qn[h % 2], QS)
                        conv_chunk(C.pbq[4 + h][:, tok], 4 + h, kn[h % 2], "pool")
                        l2n(kn[h % 2], 1.0)
                        conv_chunk(C.pbq[8 + h][:, tok], 8 + h, vs[h % 2], "pool" if h % 2 else "dve")
                    k.end_stage()
                with ExitStack() as st2:
                    sb2, ps2, sbr2, psr2 = _alloc(nc, st2)
                    sc = load_scan_consts(C, k, sb2)
                    gn = sb2("gbn", [128, 1])
                    k.dma(gn[:], C.b_normT[li])
                    oacc = [sb2("doacc%d" % h, [128, TS]) for h in range(2)]
                    S = {(d, h): [sb2("dS%d_%d_%d" % (d, h, i), [128, 128]) for i in range(2)]
                         for d in range(2) for h in (2 * hp, 2 * hp + 1)}
                    sidx = {kk: 0 for kk in S}
                    banks = [st2.enter_context(nc.psum_tensor(_uname("dps"), [128, 4, 128], F32)) for _ in range(6)]
                    k.excl.update(b.name for b in banks)
                    pss2 = psr2("dpss2", 2, [128, 512])
                    names = ("gb", "ngb", "e1", "e2", "e3", "qe", "r2", "r3", "Lm", "LT", "AT", "X0", "X1",
                             "P0", "P1", "PT0", "PT1", "kbg", "kdec", "vb", "w", "u", "vn")
                    sets = [{n_: sb2("du%d%s" % (i, n_), [128, 128]) for n_ in names} for i in range(NSET)]
                    cols = [sb2("ducol%d" % i, [128, 8]) for i in range(NSET)]
                    for h in range(2):
                        k.I("pool", "memset", W(oacc[h][:]), 0.0)
                    for kk in S:
                        k.I("pool", "memset", W(S[kk][0][:]), 0.0)
                    done = {}

                    def ready(spec):
                        n, d, h = spec
                        return n == 0 or done.get((d, h), -1) >= n - 1

                    def mark_done(spec):
                        n, d, h = spec
                        done[(d, h)] = n

                    def unit(spec, si):
                        n, d, h = spec
                        T_ = sets[si]
                        col = cols[si]
                        bk = banks[si]
                        A0, A1, D0, D1 = (bk, 0), (bk, 1), (bk, 2), (bk, 3)

                        def Wp(sl, cs=None):
                            return W(sl[0][:, sl[1], :] if cs is None else sl[0][:, sl[1], cs], sl[1])

                        def Rp(sl, cs=None, rows=None):
                            if rows is not None:
                                return R(sl[0][rows, sl[1], :], sl[1])
                            return R(sl[0][:, sl[1], :] if cs is None else sl[0][:, sl[1], cs], sl[1])
                        p = orders[d][n]
                        pc = slice(p * 128, (p + 1) * 128)
                        triA, suf = sc["triA%d" % d], sc["suf%d" % d]
                        dh = d * 4 + h
                        gcol = gg_[:, p, dh:dh + 1]
                        bcol = beta[:, p, dh:dh + 1]
                        qn_, kn_, vs_ = qn[h % 2], kn[h % 2], vs[h % 2]
                        gb, ngb, e1, e2, e3, qe = T_["gb"], T_["ngb"], T_["e1"], T_["e2"], T_["e3"], T_["qe"]
                        k.I("pool", "tensor_scalar", W(gb[:]), R(C.ones[:]), R(gcol), None, ALU.mult)
                        k.I("pool", "tensor_scalar", W(ngb[:]), R(gb[:]), -1.0, None, ALU.mult)
                        k.I("pe", "matmul", Wp(A0), R(gb[:]), R(triA[:]), start=True, stop=True)
                        k.I("pe", "matmul", Wp(A1, slice(0, 64)), R(triA[:]), R(gb[:, 0:64]), start=True, stop=True)
                        k.I("pe", "matmul", Wp(A1, slice(64, 128)), R(suf[:]), R(gb[:, 0:64]), start=True, stop=True)
                        k.I("pe", "matmul", Wp(D0), R(triA[:]), R(gb[:]), start=True, stop=False)
                        k.I("pe", "matmul", Wp(D0), R(ngb[:]), R(triA[:]), start=False, stop=True)
                        k.I("pe", "matmul", Wp(D1), R(gb[:]), R(triA[:]), start=True, stop=False)
                        k.I("pe", "matmul", Wp(D1), R(triA[:]), R(ngb[:]), start=False, stop=True)
                        yield
                        r2, r3 = T_["r2"], T_["r3"]
                        k.I("act", "activation", W(e1[:]), Rp(A0), AF.Exp)
                        k.I("act", "activation", W(col[:, 0:1]), Rp(A1, slice(0, 1)), AF.Exp)
                        k.I("act", "activation", W(col[:, 2:3]), Rp(A1, slice(64, 65)), AF.Exp)
                        if _os.environ.get("RELU", "act") == "act":
                            k.I("act", "activation", W(r2[:]), Rp(D0), AF.Relu, scale=-1.0)
                            k.I("act", "activation", W(r3[:]), Rp(D1), AF.Relu, scale=-1.0)
                        else:
                            k.I("dve", "tensor_scalar", W(r2[:]), Rp(D0), -1.0, 0.0, ALU.mult, ALU.max)
                            k.I("dve", "tensor_scalar", W(r3[:]), Rp(D1), -1.0, 0.0, ALU.mult, ALU.max)
                        k.I("act", "activation", W(e2[:]), R(r2[:]), AF.Exp, scale=-1.0)
                        k.I("act", "activation", W(e3[:]), R(r3[:]), AF.Exp, scale=-1.0)
                        k.I("pe", "matmul", Wp(D0), R(kn_[:, pc]), R(kn_[:, pc]), start=True, stop=True)
                        k.I("pe", "matmul", Wp(D1), R(kn_[:, pc]), R(qn_[:, pc]), start=True, stop=True)
                        yield
                        k.I("pool", "tensor_tensor", W(col[:, 4:5]), R(col[:, 0:1]), R(bcol), ALU.mult)
                        k.I("pool", "tensor_tensor", W(e2[:]), R(e2[:]), R(suf[:]), ALU.mult)
                        k.I("pool", "tensor_tensor", W(e3[:]), R(e3[:]), R(triA[:]), ALU.mult)
                        k.I("pool", "tensor_tensor", W(qe[:]), R(qn_[:, pc]), R(e1[:]), ALU.mult)
                        yield
                        Lm, LT, AT = T_["Lm"], T_["LT"], T_["AT"]
                        k.I("dve", "scalar_tensor_tensor", W(Lm[:]), Rp(D0), R(bcol), R(e2[:]), ALU.mult, ALU.mult)
                        k.I("dve", "tensor_tensor", W(AT[:]), Rp(D1), R(e3[:]), ALU.mult)
                        k.I("pe", "transpose", Wp(A0), R(Lm[:]), R(C.ident[:]))
                        yield
                        X = T_["X0"]
                        k.I("act", "copy", W(LT[:]), Rp(A0))
                        kbg, kdec, vb = T_["kbg"], T_["kdec"], T_["vb"]
                        yield
                        k.I("dve", "scalar_tensor_tensor", W(X[:]), R(LT[:]), -1.0, R(C.ident[:]), ALU.mult, ALU.add)
                        Pc, PTc = Lm, LT
                        k.I("pe", "matmul", Wp(A1), R(PTc[:]), R(Pc[:]), start=True, stop=True)
                        k.I("pe", "matmul", Wp(D0), R(Pc[:]), R(PTc[:]), start=True, stop=True)
                        yield
                        for it in range(5):
                            Pn, PTn = T_["P%d" % (it % 2)], T_["PT%d" % (it % 2)]
                            k.I("act", "copy", W(Pn[:]), Rp(A1))
                            if it < 4:
                                if it % 2 == 0:
                                    k.I("act", "copy", W(PTn[:]), Rp(D0))
                                else:
                                    k.I("dve", "tensor_copy", W(PTn[:]), Rp(D0))
                            yield
                            k.I("pe", "matmul", Wp(D1), R(Pn[:]), R(X[:]), start=True, stop=True)
                            if it < 4:
                                k.I("pe", "matmul", Wp(A1), R(PTn[:]), R(Pn[:]), start=True, stop=True)
                                if it < 3:
                                    k.I("pe", "matmul", Wp(D0), R(Pn[:]), R(PTn[:]), start=True, stop=True)
                            yield
                            Xn = T_["X%d" % ((it + 1) % 2)]
                            k.I("dve", "tensor_tensor", W(Xn[:]), R(X[:]), Rp(D1), ALU.add)
                            X, Pc, PTc = Xn, Pn, PTn
                        yield
                        k.I("pe", "transpose", Wp(D0), R(kn_[:, pc]), R(C.ident[:]))
                        k.I("pe", "transpose", Wp(D1), R(vs_[:, pc]), R(C.ident[:]))
                        yield
                        k.I("dve", "tensor_scalar", W(kbg[:]), Rp(D0), R(col[:, 4:5]), None, ALU.mult)
                        k.I("dve", "tensor_scalar", W(kdec[:]), Rp(D0), R(col[:, 2:3]), None, ALU.mult)
                        k.I("dve", "tensor_scalar", W(vb[:]), Rp(D1), R(bcol), None, ALU.mult)
                        yield
                        w_, u_ = T_["w"], T_["u"]
                        k.I("pe", "matmul", Wp(A0), R(kbg[:]), R(X[:]), start=True, stop=True)
                        k.I("pe", "matmul", Wp(A1), R(X[:]), R(vb[:]), start=True, stop=True)
                        yield
                        k.I("act", "copy", W(w_[:]), Rp(A0))
                        k.I("act", "copy", W(u_[:]), Rp(A1))
                        yield "S"
                        vn = T_["vn"]
                        Sl = S[(d, h)]
                        for c in ((0, 1) if d == 0 else (1, 0)):
                            cs = slice(c * 64, (c + 1) * 64)
                            Sc = Sl[sidx[(d, h)] % 2]
                            Sn = Sl[(sidx[(d, h)] + 1) % 2]
                            sidx[(d, h)] += 1
                            k.I("pe", "matmul", Wp(D0), R(w_[:]), R(Sc[:]), start=True, stop=True)
                            yield
                            k.I("dve", "scalar_tensor_tensor", W(vn[cs, :], c), Rp(D0, rows=cs), -1.0, R(u_[cs, :]),
                                ALU.mult, ALU.add)
                            yield
                            k.I("pe", "matmul", Wp(D1, cs), R(Sc[:]), R(qe[:, cs]), start=True, stop=False)
                            k.I("pe", "matmul", Wp(D1, cs), R(vn[cs, :], c), R(AT[cs, cs]), start=False, stop=True)
                            k.I("pe", "matmul", Wp(D0), R(kdec[cs, :]), R(vn[cs, :], c), start=True, stop=True)
                            yield
                            lc = c * 64 + (63 if d == 0 else 0)
                            k.I("dve", "scalar_tensor_tensor", W(Sn[:]), R(Sc[:]), R(e1[:, lc:lc + 1]), Rp(D0),
                                ALU.mult, ALU.add)
                        k.I("dve", "tensor_tensor", W(oacc[h % 2][:, pc], p), R(oacc[h % 2][:, pc], p), Rp(D1), ALU.add)

                    specs = [(n, d, h) for n in range(TPS) for d in range(2) for h in (2 * hp, 2 * hp + 1)]
                    run_units(specs, unit, NSET, ready, mark_done)
                    tmp = (sbr2("dsq", 2, [128, 512]), pss2, sbr2("drs", 2, [128, 512]), sbr2("don", 2, [128, 512]),
                           sbr2("dgg", 2, [128, 512], BF16), sbr2("dob", 2, [128, TS], BF16))
                    for h in (2 * hp, 2 * hp + 1):
                        head_norm_out(C, k, sb2, oacc[h % 2], gn[:, 0:1], C.pbz[h], C.bout[h], tok0, tmp)
                    k.end_stage()


def stage_merge(C, li):
    k, nc = C.k, C.nc
    wbr_d = C.w_branch[li]
    wout_d = C.w_out[li]
    with ExitStack() as st:
        sb, ps, sbr, psr = _alloc(nc, st)
        WA = sb("mWA", [64, 8, D], BF16)
        WB = sb("mWB", [128, 4, D], BF16)
        WC = sb("mWC", [128, 4, D], BF16)
        WO = sb("mWO", [128, 8, D], BF16)
        stg = sbr("mstg", 2, [128, D])
        g5 = [sb("mg5%d" % r, [128, D]) for r in range(3)]
        for r in range(3):
            k.dma(g5[r][:], C.mod_dram[li][r:r + 1, 5 * D:6 * D].partition_broadcast(128))
        i = 0
        for h in range(8):
            w = stg.next()
            k.dma(w[0:64, :], wbr_d[h * 64:(h + 1) * 64, :])
            k.I("pool" if i % 2 else "dve", "tensor_copy", W(WA[:, h, :], h), R(w[0:64, :]))
            i += 1
        for (Wt, base) in ((WB, 512), (WC, 1024)):
            for c in range(4):
                w = stg.next()
                k.dma(w[:], wbr_d[base + c * 128:base + (c + 1) * 128, :])
                k.I("pool" if i % 2 else "dve", "tensor_copy", W(Wt[:, c, :], c), R(w[:]))
                i += 1
        for c in range(8):
            w = stg.next()
            k.dma(w[:], wout_d[c * 128:(c + 1) * 128, :])
            k.I("pool" if i % 2 else "dve", "tensor_copy", W(WO[:, c, :], c), R(w[:]))
            i += 1
        aor = sbr("mao", 2, [64, 8, 512], BF16)
        bor = sbr("mbo", 2, [128, 4, 512], BF16)
        cor = sbr("mco", 2, [128, 4, 512], BF16)
        gtr = sbr("mgt", 2, [128, 24, 512], BF16)
        mTr = sbr("mmT", 2, [128, 8, 512], BF16)
        pA = psr("mpA", 2, [128, 512])
        pB = psr("mpB", 2, [128, 512])
        pCc = psr("mpC", 2, [128, 512])
        pY = psr("mpY", 2, [128, 512])
        t1r = sbr("mt1", 2, [128, 512])
        t2r = sbr("mt2", 2, [128, 512])
        t3r = sbr("mt3", 2, [128, 512])
        xt = sbr("mx", 2, [128, D])
        tmr = sbr("mtm", 2, [128, D])
        xor_ = sbr("mxo", 2, [128, D])
        for tb in range(NT // 512):
            ts = slice(tb * 512, (tb + 1) * 512)
            ao, bo, co, gt = aor.next(), bor.next(), cor.next(), gtr.next()
            k.dma(ao[:], C.aout[:, :, ts].rearrange("h p n -> p h n"))
            k.dma(bo[:], C.bout[:, :, ts].rearrange("h p n -> p h n"))
            k.dma(co[:], C.cout[:, :, ts].rearrange("h p n -> p h n"))
            k.dma(gt[:], C.pgate[:, :, ts].rearrange("h p n -> p h n"))
            mT = mTr.next()
            for dc in range(8):
                ds_ = slice(dc * 128, (dc + 1) * 128)
                a_ps, b_ps, c_ps = pA.next(), pB.next(), pCc.next()
                for h in range(8):
                    k.I("pe", "matmul", W(a_ps[:]), R(WA[:, h, ds_]), R(ao[:, h, :]), start=(h == 0), stop=(h == 7))
                for c in range(4):
                    k.I("pe", "matmul", W(b_ps[:]), R(WB[:, c, ds_]), R(bo[:, c, :]), start=(c == 0), stop=(c == 3))
                for c in range(4):
                    k.I("pe", "matmul", W(c_ps[:]), R(WC[:, c, ds_]), R(co[:, c, :]), start=(c == 0), stop=(c == 3))
                t1, t2, t3 = t1r.next(), t2r.next(), t3r.next()
                k.I("dve", "tensor_tensor", W(t1[:]), R(a_ps[:]), R(gt[:, dc, :]), ALU.mult)
                k.I("dve", "tensor_tensor", W(t2[:]), R(b_ps[:]), R(gt[:, 8 + dc, :]), ALU.mult)
                k.I("dve", "tensor_tensor", W(t3[:]), R(c_ps[:]), R(gt[:, 16 + dc, :]), ALU.mult)
                k.I("pool", "tensor_tensor", W(t1[:]), R(t1[:]), R(t2[:]), ALU.add)
                k.I("pool", "tensor_tensor", W(mT[:, dc, :], dc), R(t1[:]), R(t3[:]), ALU.add)
            for q in range(4):
                t = tb * 4 + q
                r = mod_row(t)
                x = xt.next()
                k.dma(x[:], C.xres[t * 128:(t + 1) * 128, :], ikey=t)
                tm = tmr.next()
                for half in range(2):
                    p = pY.next()
                    hs = slice(half * 512, (half + 1) * 512)
                    for dc in range(8):
                        k.I("pe", "matmul", W(p[:]), R(mT[:, dc, q * 128:(q + 1) * 128]), R(WO[:, dc, hs]),
                            start=(dc == 0), stop=(dc == 7))
                    k.I("dve", "tensor_tensor", W(tm[:, hs], half), R(p[:]), R(g5[r][:, hs]), ALU.mult)
                xo = xor_.next()
                k.I("pool", "tensor_tensor", W(xo[:]), R(x[:]), R(tm[:]), ALU.add)
                k.dma(C.xres[t * 128:(t + 1) * 128, :], xo[:], eng="act", okey=t)
        k.end_stage()


def host_consts():
    c = {}
    c["c_ident"] = np.eye(128, dtype=np.float32)
    perm = np.zeros((64, 64), np.float32)
    for m in range(64):
        perm[(m + 32) % 64, m] = 1.0
    c["c_perm64"] = perm
    pos = np.arange(SEQ)
    row_pos = (pos // 64).astype(np.float32)
    col_pos = (pos % 64).astype(np.float32)
    inv = (10000.0 ** (-np.arange(0, 32, 2, dtype=np.float32) / 32.0)).astype(np.float32)
    ang = np.concatenate([row_pos[:, None] * inv, col_pos[:, None] * inv], axis=-1)
    cos, sin = np.cos(ang).astype(np.float32), np.sin(ang).astype(np.float32)
    cosF = np.ones((64, TS), np.float32)
    sinS = np.zeros((64, TS), np.float32)
    cosF[0:32, CTX:] = cos.T
    cosF[32:64, CTX:] = cos.T
    sinS[0:32, CTX:] = -sin.T
    sinS[32:64, CTX:] = sin.T
    c["c_cosF"], c["c_sinS"] = cosF, sinS
    cc, rr = np.meshgrid(np.arange(128), np.arange(128), indexing="ij")
    c["c_maskLo4"] = np.tile((cc >= rr).astype(np.float32), (1, 4))
    c["c_maskHi4"] = np.tile((cc <= rr).astype(np.float32), (1, 4))
    j = np.arange(128)[:, None]
    i = np.arange(128)[None, :]
    same = (j // 64) == (i // 64)
    jl = j % 64
    f32 = np.float32
    triA0 = (same & (j <= i)).astype(f32)
    triA1 = (same & (j >= i)).astype(f32)
    c["c_triA0"], c["c_triA1"] = triA0, triA1
    c["c_triM0"] = triA0 - (same & (jl <= 31)).astype(f32)
    c["c_triM1"] = triA1 - (same & (jl >= 32)).astype(f32)
    c["c_suf0"] = (same & (j > i)).astype(f32)
    c["c_suf1"] = (same & (j < i)).astype(f32)
    return c


CONST_SHAPES = {"c_ident": [128, 128], "c_perm64": [64, 64], "c_cosF": [64, TS], "c_sinS": [64, TS],
                "c_maskLo4": [128, 512], "c_maskHi4": [128, 512],
                "c_triA0": [128, 128], "c_triA1": [128, 128], "c_triM0": [128, 128], "c_triM1": [128, 128],
                "c_suf0": [128, 128], "c_suf1": [128, 128]}


def build_program(nlayers, last_flags, upto="all", dbg=(), delta_stop=0, want_ctx_out=False):
    nc = bass.Bass("TRN2", target_bir_lowering=False)
    C = Ctx()
    C.delta_stop = delta_stop
    import os as _os
    C.unit_pool = _os.environ.get("UNITPOOL", "pool")
    C.one_unit = bool(int(_os.environ.get("ONEUNIT", "0")))
    C.nc = nc
    C.dbg = dbg

    def din(name, shape, dt=F32):
        return nc.dram_tensor(name, list(shape), dt, kind="ExternalInput").ap()

    def dscr(name, shape, dt=F32):
        kind = "ExternalOutput" if name in dbg else "Internal"
        if upto == "delta_only" and name in ("pbq", "tmba", "pbz"):
            kind = "ExternalInput"
        return nc.dram_tensor(name, list(shape), dt, kind=kind).ap()

    L = nlayers
    C.L = L
    C._din = din
    C._used = []
    C._lazy = {
        "x_in": [NSEQ, SEQ, D], "ctx_in": [NSEQ, CTX, D], "cvecT": [128, 8, 3], "c_lb": [1, DEPTH, 1024],
        "w_ada": [L, D, 9 * D], "b_ada": [L, 9 * D], "norm_g": [L, 3 * D],
        "w_ffn_gu": [L, 2, D, 2 * DFF], "w_ffn_d": [L, 2, DFF, D], "w_in": [L, D, IN_W],
        "qknT": [L, 64, 2], "a_sink": [L, 8], "c_normT": [L, 128, 1], "b_normT": [L, 128, 1],
        "convT": [L, 128, 12, 5], "b_a_log": [L, 8], "b_dt_bias": [L, 8],
        "w_branch": [L, 1536, D], "w_out": [L, D, D], "lbmask": [1, L * DEPTH],
    }
    for n_, shp_ in CONST_SHAPES.items():
        C._lazy[n_] = shp_

    class _Cst:
        def __getitem__(self, nm):
            return getattr(C, "c_" + nm)
    C.cst = _Cst()
    C.y = nc.dram_tensor("y", [NSEQ, SEQ, D], F32, kind="ExternalOutput").ap()
    C.yc = nc.dram_tensor("yc", [NSEQ, CTX, D], F32, kind="ExternalOutput").ap() if want_ctx_out else None

    C.xres = dscr("xres", [NT, D])
    C.mod_dram = [dscr("mod%d" % i, [3, 9 * D]) for i in range(L)]
    C.lb_dram = dscr("lb_dram", [1, L, 1024])
    C.actT = dscr("actT", [NF, 128, NT], BF16)
    C.pq = dscr("pq", [8, 64, NT])
    C.pk = dscr("pk", [2, 64, NT])
    C.pbq = dscr("pbq", [12, 128, NT])
    C.pbz = dscr("pbz", [4, 128, NT], BF16)
    C.pcq = dscr("pcq", [4, 128, NT])
    C.pcg = dscr("pcg", [4, 128, NT], BF16)
    C.pgate = dscr("pgate", [24, 128, NT], BF16)
    C.tmv = dscr("tmv", [NT, 128])
    C.tmba = dscr("tmba", [NT, 16])
    C.tmf = dscr("tmf", [NT, 1024])
    C.tmi = dscr("tmi", [NT, 512])
    C.aout = dscr("aout", [8, 64, NT], BF16)
    C.bout = dscr("bout", [4, 128, NT], BF16)
    C.cout = dscr("cout", [4, 128, NT], BF16)

    with ExitStack() as st:
        C.k = k = K(nc, st, nepochs=L)
        sb, ps, sbr, psr = _alloc(nc, st)
        C.scT = sb("scT", [128, 8, 3])
        C.ident = sb("ident", [128, 128])
        C.identb = sb("identb", [128, 128], BF16)
        C.ones = sb("ones", [128, 128])
        C.epsc = sb("epsc", [128, 1])
        k.I("dve", "memset", W(C.epsc[:]), EPS)
        if upto != "delta_only":
            stage_pre(C)

        def with_hT(fn):
            with nc.sbuf_tensor(_uname("hT"), [128, 8, NT], BF16) as hT:
                C.hT = hT
                fn()
            C.hT = None

        def run_layers():
            for li in range(L):
                if li > 0:
                    k.new_epoch()
                if upto == "delta_only":
                    k.I("pool", "memset", W(C.ones[:]), 1.0)
                    k.dma(C.ident[:], C.cst["ident"])
                    stage_delta(C, li)
                    return
                stage_mod(C, li)
                with_hT(lambda: (stage_norm(C, li, 0), stage_ffn_up(C, C.w_ffn_gu[li, 0])))
                stage_ffn_down(C, li, C.w_ffn_d[li, 0], 2)
                if upto == "ffn0":
                    return
                with_hT(lambda: (stage_norm(C, li, 1), stage_in_proj(C, li)))
                stage_attn(C, li)
                if upto == "attn":
                    return
                stage_gla(C, li)
                if upto == "gla":
                    return
                stage_delta(C, li)
                if upto == "mixers":
                    return
                stage_merge(C, li)
                if upto == "mix":
                    return
                with_hT(lambda: (stage_norm(C, li, 2), stage_ffn_up(C, C.w_ffn_gu[li, 1])))
                stage_ffn_down(C, li, C.w_ffn_d[li, 1], 8)
        run_layers()
        for s in range(NSEQ):
            k.dma(C.y[s], C.xres[s * TS + CTX:(s + 1) * TS, :])
            if C.yc is not None:
                k.dma(C.yc[s], C.xres[s * TS:s * TS + CTX, :])
        k.end_stage()
        print("ops", k.n_ops, "waits", k.n_waits)
    nc._used_inputs = list(C._used)
    return nc


def make_in_maps(inputs, layers, x=None, ctx=None):
    f = lambda a: np.ascontiguousarray(a, dtype=np.float32)
    c, c_ctx = inputs["c"], inputs["c_ctx"]
    x = inputs["x"] if x is None else x
    ctx = inputs["ctx"] if ctx is None else ctx
    nl = len(layers)
    shared = {
        "c_lb": f(inputs["c_lb"]).reshape(1, DEPTH, 1024),
        "w_ada": f(inputs["w_ada"][layers]),
        "b_ada": f(inputs["b_ada"][layers]),
        "norm_g": f(inputs["norm_g"][layers]).reshape(nl, 3 * D),
        "w_ffn_gu": f(inputs["w_ffn_gu"][layers]),
        "w_ffn_d": f(inputs["w_ffn_d"][layers]),
        "w_in": f(inputs["w_in"][layers]),
        "qknT": f(np.transpose(inputs["a_qk_norm"][layers], (0, 2, 1))),
        "a_sink": f(inputs["a_sink"][layers]),
        "w_branch": f(inputs["w_branch"][layers]),
        "w_out": f(inputs["w_out"][layers]),
        "lbmask": f(np.array([[1.0 if 1 <= j <= l else 0.0 for j in range(DEPTH)] for l in layers])).reshape(1, nl * DEPTH),
        "c_normT": f(inputs["c_norm"][layers]).reshape(nl, 128, 1),
        "b_normT": f(inputs["b_norm"][layers]).reshape(nl, 128, 1),
        "convT": f(np.transpose(inputs["b_conv"][layers].reshape(nl, 5, 12, 128), (0, 3, 2, 1))),
        "b_a_log": f(inputs["b_a_log"][layers]).reshape(nl, 8),
        "b_dt_bias": f(inputs["b_dt_bias"][layers]).reshape(nl, 8),
    }
    shared.update(host_consts())
    maps = []
    for core in range(NCORE):
        b0 = core * NSEQ
        cv = np.stack([c[b0], c[b0 + 1], c_ctx], axis=0)
        cvT = f(cv.reshape(3, 8, 128).transpose(2, 1, 0))
        m = dict(shared)
        m["x_in"] = f(x[b0:b0 + NSEQ])
        m["ctx_in"] = f(ctx[b0:b0 + NSEQ])
        m["cvecT"] = cvT
        maps.append(m)
    return maps


_PROG = {}


def _get_prog(key, **kw):
    if key not in _PROG:
        _PROG[key] = build_program(**kw)
    return _PROG[key]


def kernel(**inputs):
    inputs = {k_: np.asarray(v) for k_, v in inputs.items()}
    if _os.environ.get("KERNEL_MODE", "fused") == "fused":
        nc = _get_prog("l4", nlayers=DEPTH, last_flags=[False] * DEPTH, want_ctx_out=False)
        maps = make_in_maps(inputs, list(range(DEPTH)))
        maps = [{k_: v for k_, v in m.items() if k_ in nc._used_inputs} for m in maps]
        res = run_bass_kernel_spmd(nc, maps, core_ids=list(range(NCORE)))
        return np.concatenate([np.asarray(r["y"]) for r in res.results], axis=0).astype(np.float32)
    nc = _get_prog("l1", nlayers=1, last_flags=[False], want_ctx_out=True)
    x, ctx = inputs["x"], inputs["ctx"]
    for li in range(DEPTH):
        maps = make_in_maps(inputs, [li], x=x, ctx=ctx)
        maps = [{k_: v for k_, v in m.items() if k_ in nc._used_inputs} for m in maps]
        res = run_bass_kernel_spmd(nc, maps, core_ids=list(range(NCORE)))
        x = np.concatenate([np.asarray(r["y"]) for r in res.results], axis=0)
        ctx = np.concatenate([np.asarray(r["yc"]) for r in res.results], axis=0)
    return x.astype(np.float32)
```

```python
import os as _os
import numpy as np
from contextlib import ExitStack
import concourse.bass as bass
import concourse.mybir as mybir
from concourse.bass_utils import run_bass_kernel_spmd

F32 = mybir.dt.float32
BF16 = mybir.dt.bfloat16
AF = mybir.ActivationFunctionType
ALU = mybir.AluOpType
AX = mybir.AxisListType

NCORE = 8
DEPTH = 4
D = 1024
DFF = 2816
NF = DFF // 128
SEQ = 2048
CTX = 256
TS = SEQ + CTX
NSEQ = 2
NT = NSEQ * TS
NTILE = NT // 128
TPS = TS // 128
IN_W = 8464
EPS = 1e-6


class R:
    def __init__(self, ap, key=None):
        self.ap, self.key = ap, key


class W(R):
    pass


class _St:
    __slots__ = ("w", "rd")

    def __init__(self):
        self.w = None
        self.rd = {}


class K:
    ENG = ("pe", "act", "dve", "pool", "sp")
    NRING = 16

    def __init__(self, nc, stack, nepochs=1):
        self.nc = nc
        self.sems = []
        self.epochs = []
        for ep in range(nepochs):
            eng_sem = {}
            for e in self.ENG:
                eng_sem[e] = len(self.sems)
                self.sems.append(stack.enter_context(nc.semaphore("s%d_%s" % (ep, e))))
            ring = []
            for i in range(self.NRING):
                ring.append(len(self.sems))
                self.sems.append(stack.enter_context(nc.semaphore("s%d_dma%d" % (ep, i))))
            self.epochs.append((eng_sem, ring))
        self.epoch = 0
        self.eng_sem, self.ring = self.epochs[0]
        self.cnt = {e: 0 for e in self.ENG}
        self.ring_val = [0] * self.NRING
        self.dma_i = 0
        self.ops = {e: [] for e in self.ENG}
        self.known = {e: {} for e in self.ENG}
        self.state = {}
        self.excl = set()
        self.n_ops = 0
        self.n_waits = 0

    def _states(self, name, key):
        d = self.state.setdefault(name, {None: _St()})
        if key is None:
            return list(d.values())
        keys = key if isinstance(key, (list, tuple)) else [key]
        out = []
        for kk in keys:
            if kk not in d:
                s = _St()
                s.w = d[None].w
                s.rd = dict(d[None].rd)
                d[kk] = s
            out.append(d[kk])
        return out

    def _deps(self, acc):
        deps = {}
        for a in acc:
            name = a.ap.tensor.name
            ex = name in self.excl
            for st in self._states(name, None if ex else a.key):
                if st.w is not None and deps.get(st.w[0], 0) < st.w[1]:
                    deps[st.w[0]] = st.w[1]
                if ex or isinstance(a, W):
                    for s, v in st.rd.items():
                        if deps.get(s, 0) < v:
                            deps[s] = v
        return deps

    def _commit(self, acc, tok):
        s, v = tok
        for a in acc:
            name = a.ap.tensor.name
            ex = name in self.excl
            for st in self._states(name, None if ex else a.key):
                if ex or isinstance(a, W):
                    st.w = tok
                    st.rd = {}
                elif st.rd.get(s, 0) < v:
                    st.rd[s] = v

    def _emit_waits(self, eng, deps):
        kn = self.known[eng]
        for s, v in deps.items():
            if kn.get(s, 0) >= v:
                continue
            kn[s] = v
            self.ops[eng].append(("wait", s, v))
            self.n_waits += 1

    def I(self, eng, meth, *args, **kw):
        acc = [a for a in args if isinstance(a, R)] + [a for a in kw.values() if isinstance(a, R)]
        deps = self._deps(acc)
        if eng == "pe":
            deps.pop(self.eng_sem["pe"], None)
        self._emit_waits(eng, deps)
        pargs = [a.ap if isinstance(a, R) else a for a in args]
        pkw = {k_: (a.ap if isinstance(a, R) else a) for k_, a in kw.items()}
        self.cnt[eng] += 1
        tok = (self.eng_sem[eng], self.cnt[eng])
        self.ops[eng].append(("op", meth, pargs, pkw, self.eng_sem[eng], 1))
        self._commit(acc, tok)
        self.n_ops += 1

    def dma(self, out, in_, eng="sp", okey=None, ikey=None, **kw):
        acc = [W(out, okey), R(in_, ikey)]
        deps = self._deps(acc)
        i = self.dma_i
        self.dma_i += 1
        slot = i % self.NRING
        s = self.ring[slot]
        if self.ring_val[slot] > 0 and deps.get(s, 0) < self.ring_val[slot]:
            deps[s] = self.ring_val[slot]
        self._emit_waits(eng, deps)
        self.ring_val[slot] += 16
        tok = (s, self.ring_val[slot])
        self.ops[eng].append(("op", "dma_start", [], dict(out=out, in_=in_, **kw), s, 16))
        self._commit(acc, tok)
        self.n_ops += 1

    def barrier(self):
        allt = {}
        for e in self.ENG:
            if self.cnt[e] > 0:
                allt[self.eng_sem[e]] = self.cnt[e]
        for slot in range(self.NRING):
            if self.ring_val[slot] > 0:
                allt[self.ring[slot]] = self.ring_val[slot]
        for e in self.ENG:
            self._emit_waits(e, dict(allt))
        self.state = {}

    def flush(self):
        nc = self.nc
        ops = self.ops
        self.ops = {e: [] for e in self.ENG}
        sems = self.sems

        def run(engobj, lst):
            for o in lst:
                if o[0] == "wait":
                    engobj.wait_ge(sems[o[1]], o[2])
                else:
                    _, meth, pargs, pkw, s, inc = o
                    getattr(engobj, meth)(*pargs, **pkw).then_inc(sems[s], inc)

        with nc.Block() as block:
            if ops["sp"]:
                @block.sync
                def _(e):
                    run(e, ops["sp"])
            if ops["pe"]:
                @block.tensor
                def _(e):
                    run(e, ops["pe"])
            if ops["act"]:
                @block.scalar
                def _(e):
                    run(e, ops["act"])
            if ops["dve"]:
                @block.vector
                def _(e):
                    run(e, ops["dve"])
            if ops["pool"]:
                @block.gpsimd
                def _(e):
                    run(e, ops["pool"])

    def end_stage(self):
        self.barrier()
        self.flush()

    def new_epoch(self):
        self.epoch += 1
        self.eng_sem, self.ring = self.epochs[self.epoch]
        self.cnt = {e: 0 for e in self.ENG}
        self.ring_val = [0] * self.NRING
        self.known = {e: {} for e in self.ENG}
        self.state = {}


class Rot:
    def __init__(self, tiles):
        self.tiles, self.i = tiles, 0

    def next(self):
        t = self.tiles[self.i % len(self.tiles)]
        self.i += 1
        return t


class Ctx:
    def __getattr__(self, n):
        lz = self.__dict__.get("_lazy", {})
        if n in lz:
            ap = self.__dict__["_din"](n, lz[n])
            self.__dict__[n] = ap
            self.__dict__["_used"].append(n)
            return ap
        raise AttributeError(n)


_UID = [0]


def _uname(name):
    _UID[0] += 1
    return "%s_%d" % (name, _UID[0])


def _alloc(nc, st):
    def sb(name, shape, dt=F32):
        return st.enter_context(nc.sbuf_tensor(_uname(name), shape, dt))

    def ps(name, shape, dt=F32):
        return st.enter_context(nc.psum_tensor(_uname(name), shape, dt))

    def sbr(name, n, shape, dt=F32):
        return Rot([sb("%s%d" % (name, i), shape, dt) for i in range(n)])

    def psr(name, n, shape, dt=F32):
        return Rot([ps("%s%d" % (name, i), shape, dt) for i in range(n)])
    return sb, ps, sbr, psr


def mod_row(t):
    s, tt = divmod(t, TPS)
    return 2 if tt < 2 else s


def stage_pre(C):
    k, nc = C.k, C.nc
    with ExitStack() as st:
        sb, ps, sbr, psr = _alloc(nc, st)
        for s in range(NSEQ):
            k.dma(C.xres[s * TS:s * TS + CTX, :], C.ctx_in[s], okey=("c", s))
            k.dma(C.xres[s * TS + CTX:(s + 1) * TS, :], C.x_in[s], okey=("x", s))
        k.dma(C.scT[:], C.cvecT)
        k.I("act", "activation", W(C.scT[:]), R(C.scT[:]), AF.Silu)
        k.dma(C.ident[:], C.cst["ident"])
        k.I("dve", "tensor_copy", W(C.identb[:]), R(C.ident[:]))
        k.I("pool", "memset", W(C.ones[:]), 1.0)
        cl = sb("cl", [1, DEPTH, 1024])
        mx = sb("clmx", [1, 1024])
        sm = sb("clsm", [1, 1024])
        lbt = sb("lbt", [1, DEPTH, 1024])
        k.dma(cl[:], C.c_lb)
        k.I("dve", "tensor_tensor", W(mx[:]), R(cl[:, 0, :]), R(cl[:, 1, :]), ALU.max)
        for j in range(2, DEPTH):
            k.I("dve", "tensor_tensor", W(mx[:]), R(mx[:]), R(cl[:, j, :]), ALU.max)
        for j in range(DEPTH):
            k.I("dve", "tensor_tensor", W(cl[:, j, :]), R(cl[:, j, :]), R(mx[:]), ALU.subtract)
        k.I("act", "activation", W(cl[:]), R(cl[:]), AF.Exp)
        k.I("dve", "tensor_tensor", W(sm[:]), R(cl[:, 0, :]), R(cl[:, 1, :]), ALU.add)
        for j in range(2, DEPTH):
            k.I("dve", "tensor_tensor", W(sm[:]), R(sm[:]), R(cl[:, j, :]), ALU.add)
        k.I("dve", "reciprocal", W(sm[:]), R(sm[:]))
        for j in range(DEPTH):
            k.I("dve", "tensor_tensor", W(cl[:, j, :]), R(cl[:, j, :]), R(sm[:]), ALU.mult)
        lbm = sb("lbm", [1, C.L * DEPTH])
        k.dma(lbm[:], C.lbmask)
        for l_ in range(C.L):
            k.I("dve", "tensor_scalar", W(lbt[:, l_, :]), R(cl[:, 0, :]), R(lbm[:, l_ * DEPTH:l_ * DEPTH + 1]), None, ALU.mult)
            for j in range(1, DEPTH):
                k.I("dve", "scalar_tensor_tensor", W(lbt[:, l_, :]), R(cl[:, j, :]),
                    R(lbm[:, l_ * DEPTH + j:l_ * DEPTH + j + 1]), R(lbt[:, l_, :]), ALU.mult, ALU.add)
        k.dma(C.lb_dram, lbt[:, 0:C.L, :], eng="act")
        k.end_stage()


def stage_mod(C, li):
    k, nc = C.k, C.nc
    with ExitStack() as st:
        sb, ps, sbr, psr = _alloc(nc, st)
        wt = sbr("wada", 2, [128, 8, 512])
        pm = psr("pmod", 2, [3, 512])
        msb = sb("msb", [3, 9 * D])
        bsb = sb("bsb", [3, 9 * D])
        gsb = sb("gsb", [3, 3 * D])
        k.dma(bsb[:], C.b_ada[li:li + 1, :].partition_broadcast(3))
        k.dma(gsb[:], C.norm_g[li:li + 1].partition_broadcast(3))
        for j in range(18):
            w = wt.next()
            k.dma(w[:], C.w_ada[li][:, j * 512:(j + 1) * 512].rearrange("(c p) n -> p c n", p=128))
            p = pm.next()
            for c in range(8):
                k.I("pe", "matmul", W(p[:]), R(C.scT[:, c, :]), R(w[:, c, :]), start=(c == 0), stop=(c == 7))
            k.I("dve", "tensor_tensor", W(msb[:, j * 512:(j + 1) * 512], key=j), R(p[:]),
                R(bsb[:, j * 512:(j + 1) * 512]), ALU.add)
        for jn in range(3):
            sl = slice((3 * jn + 1) * D, (3 * jn + 2) * D)
            k.I("dve", "scalar_tensor_tensor", W(msb[:, sl]), R(msb[:, sl]), 1.0, R(gsb[:, jn * D:(jn + 1) * D]), ALU.add, ALU.mult)
        for idx in (2, 8):
            sl = slice(idx * D, (idx + 1) * D)
            k.I("dve", "tensor_scalar", W(msb[:, sl]), R(msb[:, sl]), 0.5, None, ALU.mult)
        k.dma(C.mod_dram[li], msb[:], eng="act")
        k.end_stage()


def stage_norm(C, li, jn):
    k, nc = C.k, C.nc
    with ExitStack() as st:
        sb, ps, sbr, psr = _alloc(nc, st)
        gs = [sb("gs%d" % r, [128, D]) for r in range(3)]
        sh = [sb("sh%d" % r, [128, D]) for r in range(3)]
        for r in range(3):
            k.dma(gs[r][:], C.mod_dram[li][r:r + 1, (3 * jn + 1) * D:(3 * jn + 2) * D].partition_broadcast(128))
            k.dma(sh[r][:], C.mod_dram[li][r:r + 1, (3 * jn) * D:(3 * jn + 1) * D].partition_broadcast(128))
        xt = sbr("nx", 4, [128, D])
        junk = sbr("njunk", 3, [128, D], BF16)
        ssr = sbr("nss", 4, [128, 1])
        h1r = sbr("nh1", 4, [128, D])
        hbr = sbr("nhb", 4, [128, D], BF16)
        ptr = psr("nptr", 4, [128, 8, 128], BF16)
        for t in range(NTILE):
            r = mod_row(t)
            x = xt.next()
            k.dma(x[:], C.xres[t * 128:(t + 1) * 128, :], ikey=t)
            ss = ssr.next()
            jk = junk.next()
            k.I("dve", "memset", W(ss[:]), 0.0)
            k.I("act", "activation", W(jk[:]), R(x[:]), AF.Square, accum_out=W(ss[:]))
            k.I("act", "activation", W(ss[:]), R(ss[:]), AF.Sqrt, bias=R(C.epsc[:]), scale=1.0 / D)
            k.I("dve", "reciprocal", W(ss[:]), R(ss[:]))
            h1 = h1r.next()
            k.I("dve", "scalar_tensor_tensor", W(h1[:]), R(x[:]), R(ss[:, 0:1]), R(gs[r][:]), ALU.mult, ALU.mult)
            hb = hbr.next()
            k.I("pool", "tensor_tensor", W(hb[:]), R(h1[:]), R(sh[r][:]), ALU.add)
            pt = ptr.next()
            for c in range(8):
                k.I("pe", "transpose", W(pt[:, c, :], key=c), R(hb[:, c * 128:(c + 1) * 128]), R(C.identb[:]))
            k.I("act", "copy", W(C.hT[:, :, t * 128:(t + 1) * 128], key=t), R(pt[:]))
        k.end_stage()


def stage_ffn_up(C, wgu):
    k, nc = C.k, C.nc
    with ExitStack() as st:
        sb, ps, sbr, psr = _alloc(nc, st)
        w32 = sbr("fw32", 3, [128, 8, 256])
        wbr = sbr("fwb", 3, [128, 8, 256], BF16)
        pg = psr("fpg", 2, [128, 512])
        pu = psr("fpu", 2, [128, 512])
        sgr = sbr("fsg", 3, [128, 512])
        abr = sbr("fab", 2, [128, NT], BF16)
        for f in range(NF):
            w = w32.next()
            k.dma(w[:, :, 0:128], wgu[:, f * 128:(f + 1) * 128].rearrange("(c p) n -> p c n", p=128))
            k.dma(w[:, :, 128:256], wgu[:, DFF + f * 128:DFF + (f + 1) * 128].rearrange("(c p) n -> p c n", p=128))
            wb = wbr.next()
            k.I("pool", "tensor_copy", W(wb[:]), R(w[:]))
            ab = abr.next()
            for tb in range(NT // 512):
                g = pg.next()
                u = pu.next()
                ts = slice(tb * 512, (tb + 1) * 512)
                for c in range(8):
                    k.I("pe", "matmul", W(g[:]), R(wb[:, c, 0:128]), R(C.hT[:, c, ts]), start=(c == 0), stop=(c == 7))
                for c in range(8):
                    k.I("pe", "matmul", W(u[:]), R(wb[:, c, 128:256]), R(C.hT[:, c, ts]), start=(c == 0), stop=(c == 7))
                sg = sgr.next()
                k.I("act", "activation", W(sg[:]), R(g[:]), AF.Silu)
                k.I("dve", "tensor_tensor", W(ab[:, ts], key=tb), R(sg[:]), R(u[:]), ALU.mult)
            k.dma(C.actT[f], ab[:], eng="act", okey=f)
        k.end_stage()


def stage_ffn_down(C, li, wd, gidx):
    k, nc = C.k, C.nc
    with ExitStack() as st:
        sb, ps, sbr, psr = _alloc(nc, st)
        wdb = sb("dwb", [128, NF, D], BF16)
        w32 = sbr("dw32", 2, [128, D])
        gb = [sb("dgb%d" % r, [128, D]) for r in range(3)]
        for r in range(3):
            k.dma(gb[r][:], C.mod_dram[li][r:r + 1, gidx * D:(gidx + 1) * D].partition_broadcast(128))
        for f2 in range(NF):
            w = w32.next()
            k.dma(w[:], wd[f2 * 128:(f2 + 1) * 128, :])
            k.I("pool" if f2 % 2 else "dve", "tensor_copy", W(wdb[:, f2, :], key=f2), R(w[:]))
        abr = sbr("dab", 2, [128, NF, 256], BF16)
        xt = sbr("dx", 3, [128, D])
        tmr = sbr("dtm", 3, [128, D])
        xor_ = sbr("dxo", 3, [128, D])
        py = psr("dpy", 4, [128, 512])
        for tb in range(NT // 256):
            a = abr.next()
            k.dma(a[:], C.actT[:, :, tb * 256:(tb + 1) * 256].rearrange("f p n -> p f n"))
            for q in range(2):
                t = tb * 2 + q
                r = mod_row(t)
                x = xt.next()
                k.dma(x[:], C.xres[t * 128:(t + 1) * 128, :], ikey=t)
                tm = tmr.next()
                for half in range(2):
                    p = py.next()
                    hs = slice(half * 512, (half + 1) * 512)
                    for f in range(NF):
                        k.I("pe", "matmul", W(p[:]), R(a[:, f, q * 128:(q + 1) * 128]), R(wdb[:, f, hs]),
                            start=(f == 0), stop=(f == NF - 1))
                    k.I("dve", "tensor_tensor", W(tm[:, hs], key=half), R(p[:]), R(gb[r][:, hs]), ALU.mult)
                xo = xor_.next()
                k.I("pool", "tensor_tensor", W(xo[:]), R(x[:]), R(tm[:]), ALU.add)
                k.dma(C.xres[t * 128:(t + 1) * 128, :], xo[:], eng="act", okey=t)
        k.end_stage()


OQ, OK_, OV = 0, 512, 640
OBQ, OBZ, OBB, OBA = 768, 2304, 2816, 2824
OCQ, OCF, OCI, OCG = 2832, 3344, 4368, 4880
OG = 5392
TMW = 1680


def stage_in_proj(C, li):
    k, nc = C.k, C.nc
    win = C.w_in[li]
    with ExitStack() as st:
        sb, ps, sbr, psr = _alloc(nc, st)
        wtm = sb("wtm", [128, 8, TMW], BF16)
        stg = sbr("wtmst", 2, [128, 8, 256])
        segs = [(OV, 128, 0), (OBB, 16, 128)] + [(OCF + i * 256, 256, 144 + i * 256) for i in range(4)] \
            + [(OCI + i * 256, 256, 1168 + i * 256) for i in range(2)]
        for i, (c0, wd_, d0) in enumerate(segs):
            w = stg.next()
            k.dma(w[:, :, 0:wd_], win[:, c0:c0 + wd_].rearrange("(c p) n -> p c n", p=128))
            k.I("pool" if i % 2 else "dve", "tensor_copy", W(wtm[:, :, d0:d0 + wd_], key=i), R(w[:, :, 0:wd_]))
        pA = psr("ipA", 1, [128, 144])
        pB = psr("ipB", 3, [128, 512])
        ot = sbr("iot", 2, [128, TMW])
        for t in range(NTILE):
            o = ot.next()
            tsl = slice(t * 128, (t + 1) * 128)
            groups = [(0, 144, pA.next()), (144, 512, pB.next()), (656, 512, pB.next()), (1168, 512, pB.next())]
            for gi, (c0, n, p) in enumerate(groups):
                for c in range(8):
                    k.I("pe", "matmul", W(p[:, 0:n]), R(C.hT[:, c, tsl]), R(wtm[:, c, c0:c0 + n]),
                        start=(c == 0), stop=(c == 7))
                if gi % 2:
                    k.I("act", "copy", W(o[:, c0:c0 + n], key=gi), R(p[:, 0:n]))
                else:
                    k.I("dve", "tensor_copy", W(o[:, c0:c0 + n], key=gi), R(p[:, 0:n]))
            k.dma(C.tmv[tsl, :], o[:, 0:128], eng="act")
            k.dma(C.tmba[tsl, :], o[:, 128:144], eng="act")
            k.dma(C.tmf[tsl, :], o[:, 144:1168], eng="act")
            k.dma(C.tmi[tsl, :], o[:, 1168:1680], eng="act")
        chunks = []
        for h in range(8):
            chunks.append((OQ + h * 64, 64, C.pq[h], "copy"))
        for h in range(2):
            chunks.append((OK_ + h * 64, 64, C.pk[h], "copy"))
        for c in range(12):
            chunks.append((OBQ + c * 128, 128, C.pbq[c], "copy"))
        for c in range(4):
            chunks.append((OBZ + c * 128, 128, C.pbz[c], "silu"))
        for c in range(4):
            chunks.append((OCQ + c * 128, 128, C.pcq[c], "silu32"))
        for c in range(4):
            chunks.append((OCG + c * 128, 128, C.pcg[c], "silu"))
        for c in range(24):
            chunks.append((OG + c * 128, 128, C.pgate[c], "sigmoid"))
        w32 = sbr("iw32", 2, [128, 8, 128])
        wbr = sbr("iwb", 2, [128, 8, 128], BF16)
        pp = psr("ipp", 4, [128, 512])
        o32 = sbr("io32", 2, [128, NT])
        o16 = sbr("io16", 2, [128, NT], BF16)
        for ci, (c0, m, dst, kind) in enumerate(chunks):
            w = w32.next()
            k.dma(w[:, :, 0:m], win[:, c0:c0 + m].rearrange("(c p) n -> p c n", p=128))
            wb = wbr.next()
            k.I("pool", "tensor_copy", W(wb[:, :, 0:m]), R(w[:, :, 0:m]))
            ob = o32.next() if kind in ("copy", "silu32") else o16.next()
            for tb in range(NT // 512):
                p = pp.next()
                ts = slice(tb * 512, (tb + 1) * 512)
                for c in range(8):
                    k.I("pe", "matmul", W(p[0:m, :]), R(wb[:, c, 0:m]), R(C.hT[:, c, ts]), start=(c == 0), stop=(c == 7))
                if kind == "copy":
                    if tb % 2:
                        k.I("act", "copy", W(ob[0:m, ts], key=tb), R(p[0:m, :]))
                    else:
                        k.I("dve", "tensor_copy", W(ob[0:m, ts], key=tb), R(p[0:m, :]))
                elif kind == "sigmoid":
                    k.I("act", "activation", W(ob[0:m, ts], key=tb), R(p[0:m, :]), AF.Sigmoid)
                else:
                    k.I("act", "activation", W(ob[0:m, ts], key=tb), R(p[0:m, :]), AF.Silu)
            k.dma(dst, ob[0:m, :], eng="act")
        k.end_stage()


def stage_attn(C, li):
    k, nc = C.k, C.nc
    SCALE = 0.125
    with ExitStack() as st:
        sb, ps, sbr, psr = _alloc(nc, st)
        cosF = sb("cosF", [64, TS])
        sinS = sb("sinS", [64, TS])
        perm = sb("perm", [64, 64])
        mlo = sb("mlo", [128, 4, 128], BF16)
        mhi = sb("mhi", [128, 4, 128], BF16)
        m32 = sb("m32", [128, 4, 128])
        gqk = sb("gqk", [64, 2])
        se = sb("sinke", [64, 8])
        sk = sb("sk", [64, 8, 128])
        onesb = sb("onesb", [128, 64], BF16)
        k.dma(cosF[:], C.cst["cosF"])
        k.dma(sinS[:], C.cst["sinS"])
        k.dma(perm[:], C.cst["perm64"])
        k.dma(m32[:], C.cst["maskLo4"].rearrange("p (h n) -> p h n", h=4))
        k.I("dve", "tensor_copy", W(mlo[:]), R(m32[:]))
        k.dma(m32[:], C.cst["maskHi4"].rearrange("p (h n) -> p h n", h=4))
        k.I("dve", "tensor_copy", W(mhi[:]), R(m32[:]))
        k.dma(gqk[:], C.qknT[li])
        k.dma(se[:], C.a_sink[li:li + 1, :].partition_broadcast(64))
        k.I("act", "activation", W(se[:]), R(se[:]), AF.Exp)
        for h in range(8):
            k.I("dve", "tensor_scalar", W(sk[:, h, :], key=h), R(C.ones[0:64, 0:128]), R(se[:, h:h + 1]), None, ALU.mult)
        k.I("dve", "tensor_copy", W(onesb[:]), R(C.ones[:, 0:64]))

        raw = sbr("araw", 2, [64, TS])
        qT = sb("aqT", [64, 8, TS], BF16)
        kT = sb("akT", [64, 2, TS], BF16)
        v32 = sb("av32", [128, TPS, 128])
        vb = sb("avb", [128, TPS, 128], BF16)
        sqr = sbr("asq", 3, [64, 512])
        rsr = sbr("ars", 3, [64, 512])
        knr = sbr("akn", 3, [64, 512])
        t1r = sbr("at1", 3, [64, 512])
        t2r = sbr("at2", 3, [64, 512])
        pss = psr("apss", 1, [64, 512])
        ppm = psr("appm", 1, [64, 512])
        pst = psr("apst", 2, [128, 4, 128])
        po = psr("apo", 2, [64, 4, 128])
        pd = psr("apd", 2, [64, 4, 128])
        Pr = sbr("aP", 3, [128, 4, 128], BF16)
        Pm = sbr("aPm", 2, [128, 4, 128], BF16)
        rdr = sbr("ard", 2, [64, 4, 128])
        obr = sbr("aob", 2, [64, 4, TS], BF16)
        blocks = [(i * 512, 512) for i in range(4)] + [(2048, 256)]

        def prep(src_dram, gcol, dst):
            r = raw.next()
            k.dma(r[:], src_dram)
            for (b0, n) in blocks:
                cs = slice(b0, b0 + n)
                sq = sqr.next()
                k.I("act", "activation", W(sq[:, 0:n]), R(r[:, cs]), AF.Square)
                p1 = pss.next()
                k.I("pe", "matmul", W(p1[:, 0:n]), R(C.ones[0:64, 0:64]), R(sq[:, 0:n]), start=True, stop=True)
                rs = rsr.next()
                k.I("act", "activation", W(rs[:, 0:n]), R(p1[:, 0:n]), AF.Sqrt, bias=R(C.epsc[0:64, :]), scale=1.0 / 64)
                k.I("dve", "reciprocal", W(rs[:, 0:n]), R(rs[:, 0:n]))
                kn = knr.next()
                k.I("dve", "scalar_tensor_tensor", W(kn[:, 0:n]), R(r[:, cs]), R(gcol), R(rs[:, 0:n]), ALU.mult, ALU.mult)
                p2 = ppm.next()
                k.I("pe", "matmul", W(p2[:, 0:n]), R(perm[:]), R(kn[:, 0:n]), start=True, stop=True)
                t1 = t1r.next()
                k.I("pool", "tensor_tensor", W(t1[:, 0:n]), R(kn[:, 0:n]), R(cosF[:, cs]), ALU.mult)
                t2 = t2r.next()
                k.I("dve", "tensor_tensor", W(t2[:, 0:n]), R(p2[:, 0:n]), R(sinS[:, cs]), ALU.mult)
                k.I("pool", "tensor_tensor", W(dst[:, cs]), R(t1[:, 0:n]), R(t2[:, 0:n]), ALU.add)

        for s in range(NSEQ):
            tok = slice(s * TS, (s + 1) * TS)
            for g in range(2):
                prep(C.pk[g][:, tok], gqk[:, 1:2], kT[:, g, :])
            for h in range(8):
                prep(C.pq[h][:, tok], gqk[:, 0:1], qT[:, h, :])
            k.dma(v32[:], C.tmv[tok, :].rearrange("(t p) n -> p t n", p=128))
            k.I("dve", "tensor_copy", W(vb[:]), R(v32[:]))
            for g in range(2):
                ob = obr.next()
                for qt in range(TPS):
                    if qt < 2:
                        kcs = [(0, None), (1, None)]
                    else:
                        kcs = [(0, None), (1, None)]
                        if qt - 1 >= 2:
                            kcs.append((qt - 1, mlo))
                        kcs.append((qt, None))
                        if qt + 1 < TPS:
                            kcs.append((qt + 1, mhi))
                    o_ps = po.next()
                    d_ps = pd.next()
                    qs = slice(qt * 128, (qt + 1) * 128)
                    def score(kt, msk):
                        s_ps = pst.next()
                        k.I("pe", "matmul", W(s_ps[:]), R(kT[:, g, kt * 128:(kt + 1) * 128]), R(qT[:, 4 * g:4 * g + 4, qs]),
                            start=True, stop=True)
                        P = Pr.next()
                        k.I("act", "activation", W(P[:]), R(s_ps[:]), AF.Exp, scale=SCALE)
                        if msk is not None:
                            P2 = Pm.next()
                            k.I("dve", "tensor_tensor", W(P2[:]), R(P[:]), R(msk[:]), ALU.mult)
                            P = P2
                        return P
                    Pcur = score(*kcs[0])
                    for i, (kt, msk) in enumerate(kcs):
                        Pnext = score(*kcs[i + 1]) if i + 1 < len(kcs) else None
                        k.I("pe", "matmul", W(o_ps[:]), R(vb[:, kt, g * 64:(g + 1) * 64]), R(Pcur[:]),
                            start=(i == 0), stop=(i == len(kcs) - 1))
                        k.I("pe", "matmul", W(d_ps[:]), R(onesb[:]), R(Pcur[:]),
                            start=(i == 0), stop=(i == len(kcs) - 1))
                        Pcur = Pnext
                    rd = rdr.next()
                    k.I("dve", "tensor_tensor", W(rd[:]), R(d_ps[:]), R(sk[:, 4 * g:4 * g + 4, :]), ALU.add)
                    k.I("dve", "reciprocal", W(rd[:]), R(rd[:]))
                    k.I("dve", "tensor_tensor", W(ob[:, :, qs], key=qt), R(o_ps[:]), R(rd[:]), ALU.mult)
                k.dma(C.aout[4 * g:4 * g + 4, :, tok].rearrange("h p n -> p h n"), ob[:], eng="act")
        k.end_stage()


class PSlots:
    def __init__(self, nc, st, nbanks, name, k=None):
        self.banks = [st.enter_context(nc.psum_tensor(_uname(name), [128, 4, 128], F32)) for _ in range(nbanks)]
        if k is not None:
            k.excl.update(b.name for b in self.banks)
        self.i = 0

    def next(self):
        n = len(self.banks) * 4
        j = self.i % n
        self.i += 1
        b, q = j % len(self.banks), j // len(self.banks)
        return self.banks[b], q


def scan_orders():
    fwd = list(range(TPS))
    bwd = [1, 0] + list(range(TPS - 1, 1, -1))
    return fwd, bwd


def head_norm_out(C, k, sb_, oacc, gcol, gate_dram, dst_dram, tok0, tmp):
    sqr, pss, rsr, onr, ggr, obf = tmp
    ob = obf.next()
    for (b0, n) in [(i * 512, 512) for i in range(4)] + [(2048, 256)]:
        cs = slice(b0, b0 + n)
        sq = sqr.next()
        k.I("act", "activation", W(sq[:, 0:n]), R(oacc[:, cs]), AF.Square)
        p1 = pss.next()
        k.I("pe", "matmul", W(p1[:, 0:n]), R(C.ones[:]), R(sq[:, 0:n]), start=True, stop=True)
        rs = rsr.next()
        k.I("act", "activation", W(rs[:, 0:n]), R(p1[:, 0:n]), AF.Sqrt, bias=R(C.epsc[:]), scale=1.0 / 128)
        k.I("dve", "reciprocal", W(rs[:, 0:n]), R(rs[:, 0:n]))
        on = onr.next()
        k.I("dve", "scalar_tensor_tensor", W(on[:, 0:n]), R(oacc[:, cs]), R(gcol), R(rs[:, 0:n]), ALU.mult, ALU.mult)
        gg = ggr.next()
        k.dma(gg[:, 0:n], gate_dram[:, tok0 + b0:tok0 + b0 + n])
        k.I("pool", "tensor_tensor", W(ob[:, cs], key=b0), R(on[:, 0:n]), R(gg[:, 0:n]), ALU.mult)
    k.dma(dst_dram[:, tok0:tok0 + TS], ob[:], eng="act")


def load_scan_consts(C, k, sb):
    cs = {}
    for n in ("triA0", "triA1", "triM0", "triM1", "suf0", "suf1"):
        t = sb(n, [128, 128])
        k.dma(t[:], C.cst[n])
        cs[n] = t
    return cs


def stage_gla(C, li):
    k, nc = C.k, C.nc
    QS = 128 ** -0.5
    NSET = 6
    with ExitStack() as st:
        sb, ps, sbr, psr = _alloc(nc, st)
        sc = load_scan_consts(C, k, sb)
        lb = [sb("lb%d" % d, [128, 512]) for d in range(2)]
        oml = [sb("oml%d" % d, [128, 512]) for d in range(2)]
        for d in range(2):
            k.dma(lb[d][:], C.lb_dram[0:1, li, d * 512:(d + 1) * 512].partition_broadcast(128))
            k.I("dve", "tensor_scalar", W(oml[d][:]), R(lb[d][:]), -1.0, 1.0, ALU.mult, ALU.add)
        gn = sb("gcn", [128, 1])
        k.dma(gn[:], C.c_normT[li])
        banks = [st.enter_context(nc.psum_tensor(_uname("gps"), [128, 4, 128], F32)) for _ in range(NSET)]
        k.excl.update(b.name for b in banks)
        pss = psr("gpss", 2, [128, 512])
        ar = sbr("ga", 2, [128, 512])
        sgr = sbr("gsg", 2, [128, 512])
        t1r = sbr("gt1", 2, [128, 512])
        fr = sbr("gf", 2, [128, 512])
        vir = sbr("gvi", 4, [128, 512])
        qfr = sbr("gqf", 4, [128, 4, 128])
        ktr = sbr("gkt", 4, [128, 512])
        lfr = sbr("glf", 4, [128, 512])
        names = ("e1", "e2", "e3", "e4", "qe", "qt", "ktl", "kh", "at0", "at1")
        sets = [{n_: sb("gu%d%s" % (i, n_), [128, 128]) for n_ in names} for i in range(NSET)]
        mku = [sb("gmk%d" % d, [128, 128], mybir.dt.uint32) for d in range(2)]
        for d in range(2):
            k.I("dve", "tensor_scalar", W(mku[d][:]), R(sc["triA%d" % d][:]), 0.5, None, ALU.is_gt)
        for i in range(NSET):
            k.I("pool", "memset", W(sets[i]["at0"][:]), 0.0)
            k.I("pool", "memset", W(sets[i]["at1"][:]), 0.0)
        S = {(d, h): [sb("gS%d_%d_%d" % (d, h, i), [128, 128]) for i in range(2)] for d in range(2) for h in range(4)}
        oacc = [sb("goacc%d" % h, [128, TS]) for h in range(4)]
        tmp = (sbr("hsq", 3, [128, 512]), pss, sbr("hrs", 3, [128, 512]), sbr("hon", 3, [128, 512]),
               sbr("hgg", 3, [128, 512], BF16), sbr("hob", 2, [128, TS], BF16))
        orders = scan_orders()
        for s in range(NSEQ):
            tok0 = s * TS
            for h in range(4):
                k.I("pool", "memset", W(oacc[h][:]), 0.0)
            sidx = {}
            for kk in S:
                k.I("pool", "memset", W(S[kk][0][:]), 0.0)
                sidx[kk] = 0
            done = {}
            shared = {}

            def ready(spec):
                n, d, h = spec
                return n == 0 or done.get((d, h), -1) >= n - 1

            def mark_done(spec):
                n, d, h = spec
                done[(d, h)] = n

            def unit(spec, si):
                n, d, h = spec
                T_ = sets[si]
                bk = banks[si]

                def Pw(q, cs=None):
                    return W(bk[:, q, :] if cs is None else bk[:, q, cs])

                def Pr(q):
                    return W(bk[:, q, :])
                p = orders[d][n]
                rows = slice(tok0 + p * 128, tok0 + (p + 1) * 128)
                pc = slice(p * 128, (p + 1) * 128)
                triA, triM, suf = sc["triA%d" % d], sc["triM%d" % d], sc["suf%d" % d]
                if h == 0:
                    a = ar.next()
                    k.dma(a[:], C.tmf[rows, d * 512:(d + 1) * 512])
                    vi = vir.next()
                    k.dma(vi[:], C.tmi[rows, :])
                    qf = qfr.next()
                    k.dma(qf[:], C.pcq[:, :, rows].rearrange("h p n -> p h n"))
                    k.I("pool", "tensor_scalar", W(qf[:]), R(qf[:]), QS, None, ALU.mult)
                    sg = sgr.next()
                    k.I("act", "activation", W(sg[:]), R(a[:]), AF.Sigmoid)
                    t1 = t1r.next()
                    k.I("dve", "tensor_tensor", W(t1[:]), R(sg[:]), R(oml[d][:]), ALU.mult)
                    f = fr.next()
                    k.I("pool", "tensor_tensor", W(f[:]), R(t1[:]), R(lb[d][:]), ALU.add)
                    kt = ktr.next()
                    k.I("pool", "tensor_tensor", W(kt[:]), R(oml[d][:]), R(t1[:]), ALU.subtract)
                    lf = lfr.next()
                    k.I("act", "activation", W(lf[:]), R(f[:]), AF.Ln)
                    shared[(n, d)] = (vi, qf, kt, lf)
                    yield
                vi, qf, kt, lf = shared[(n, d)]
                hs = slice(h * 128, (h + 1) * 128)
                e1, e2, e3, e4 = T_["e1"], T_["e2"], T_["e3"], T_["e4"]
                qe, qt_, ktl, kh, at = T_["qe"], T_["qt"], T_["ktl"], T_["kh"], T_["at%d" % d]
                k.I("pe", "matmul", Pw(0), R(lf[:, hs]), R(triA[:]), start=True, stop=True)
                k.I("pe", "matmul", Pw(1), R(lf[:, hs]), R(triM[:]), start=True, stop=True)
                k.I("pe", "matmul", Pw(2), R(suf[:]), R(lf[:, hs]), start=True, stop=True)
                k.I("pe", "transpose", Pw(3), R(kt[:, hs]), R(C.ident[:]))
                yield
                k.I("act", "activation", W(e1[:]), Pr(0), AF.Exp)
                k.I("act", "activation", W(e2[:]), Pr(1), AF.Exp)
                k.I("act", "activation", W(e3[:]), Pr(1), AF.Exp, scale=-1.0)
                k.I("act", "activation", W(e4[:]), Pr(2), AF.Exp)
                yield
                k.I("dve", "tensor_tensor", W(qe[:]), R(qf[:, h, :]), R(e1[:]), ALU.mult)
                k.I("pool", "tensor_tensor", W(qt_[:]), R(qf[:, h, :]), R(e2[:]), ALU.mult)
                k.I("dve", "tensor_tensor", W(ktl[:]), Pr(3), R(e3[:]), ALU.mult)
                k.I("pool", "tensor_tensor", W(kh[:]), R(kt[:, hs]), R(e4[:]), ALU.mult)
                yield
                k.I("pe", "matmul", Pw(0), R(ktl[:]), R(qt_[:]), start=True, stop=True)
                yield
                k.I("dve", "copy_predicated", W(at[:]), R(mku[d][:]), Pr(0))
                yield "S"
                Sl = S[(d, h)]
                for c in ((0, 1) if d == 0 else (1, 0)):
                    cs = slice(c * 64, (c + 1) * 64)
                    Sc = Sl[sidx[(d, h)] % 2]
                    Sn = Sl[(sidx[(d, h)] + 1) % 2]
                    sidx[(d, h)] += 1
                    k.I("pe", "matmul", Pw(1, cs), R(Sc[:]), R(qe[:, cs]), start=True, stop=False)
                    k.I("pe", "matmul", Pw(1, cs), R(vi[cs, hs]), R(at[cs, cs]), start=False, stop=True)
                    k.I("pe", "matmul", Pw(2), R(kh[cs, :]), R(vi[cs, hs]), start=True, stop=True)
                    yield
                    lc = c * 64 + (63 if d == 0 else 0)
                    k.I("dve", "scalar_tensor_tensor", W(Sn[:]), R(Sc[:]), R(e1[:, lc:lc + 1]), Pr(2), ALU.mult, ALU.add)
                    yield
                k.I("dve", "tensor_tensor", W(oacc[h][:, pc], p), R(oacc[h][:, pc], p), Pr(1), ALU.add)

            specs = [(n, d, h) for n in range(TPS) for d in range(2) for h in range(4)]
            run_units(specs, unit, NSET, ready, mark_done)
            for h in range(4):
                head_norm_out(C, k, sb, oacc[h], gn[:, 0:1], C.pcg[h], C.cout[h], tok0, tmp)
        k.end_stage()


def run_units(specs, make_gen, nset, ready, mark_done):
    free = list(range(nset))
    active = []
    it = iter(specs)
    exhausted = False
    while True:
        while free and not exhausted:
            spec = next(it, None)
            if spec is None:
                exhausted = True
                break
            si = free.pop(0)
            active.append([make_gen(spec, si), si, spec, False])
        if not active:
            break
        for a in list(active):
            if a[3]:
                if not ready(a[2]):
                    continue
                a[3] = False
            try:
                r = next(a[0])
                if r == "S" and not ready(a[2]):
                    a[3] = True
            except StopIteration:
                mark_done(a[2])
                active.remove(a)
                free.append(a[1])


def stage_delta(C, li):
    k, nc = C.k, C.nc
    QS = 128 ** -0.5
    NSET = int(_os.environ.get("NSET", "6"))
    orders = scan_orders()
    blocks = [(i * 512, 512) for i in range(4)] + [(2048, 256)]
    for s in range(NSEQ):
        tok0 = s * TS
        tok = slice(tok0, tok0 + TS)
        for hp in range(2):
            with ExitStack() as st:
                sb, ps, sbr, psr = _alloc(nc, st)
                qn = [sb("dqn%d" % h, [128, TS]) for h in range(2)]
                kn = [sb("dkn%d" % h, [128, TS]) for h in range(2)]
                vs = [sb("dvs%d" % h, [128, TS]) for h in range(2)]
                beta = sb("dbeta", [128, TPS, 8])
                gg_ = sb("dg", [128, TPS, 8])
                with ExitStack() as st1:
                    sb1, ps1, sbr1, psr1 = _alloc(nc, st1)
                    convw = sb1("convw", [128, 12, 5])
                    k.dma(convw[:], C.convT[li])
                    nala = sb1("nala", [128, 8])
                    dtb = sb1("dtb", [128, 8])
                    k.dma(nala[:], C.b_a_log[li:li + 1, :].partition_broadcast(128))
                    k.dma(dtb[:], C.b_dt_bias[li:li + 1, :].partition_broadcast(128))
                    k.I("act", "activation", W(nala[:]), R(nala[:]), AF.Exp)
                    k.I("dve", "tensor_scalar", W(nala[:]), R(nala[:]), -1.0, None, ALU.mult)
                    ba = sb1("dba", [128, TPS, 16])
                    pss = psr1("dpss", 2, [128, 512])
                    xraw = sbr1("dxraw", 2, [128, TS])
                    acc = sbr1("dacc", 2, [128, TS])
                    ctmp = sbr1("dctmp", 1, [128, TS])
                    sqr = sbr1("dsq", 3, [128, 512])
                    rsr = sbr1("drs", 3, [128, 512])
                    k.dma(ba[:], C.tmba[tok, :].rearrange("(t p) n -> p t n", p=128))
                    k.I("act", "activation", W(beta[:]), R(ba[:, :, 0:8]), AF.Sigmoid)
                    for t in range(TPS):
                        k.I("dve", "tensor_tensor", W(gg_[:, t, :]), R(ba[:, t, 8:16]), R(dtb[:]), ALU.add)
                    k.I("act", "activation", W(gg_[:]), R(gg_[:]), AF.Exp)
                    k.I("act", "activation", W(gg_[:]), R(gg_[:]), AF.Ln, bias=R(C.ones[:, 0:1]), scale=1.0)
                    for t in range(TPS):
                        k.I("dve", "tensor_tensor", W(gg_[:, t, :]), R(gg_[:, t, :]), R(nala[:]), ALU.mult)

                    def conv_chunk(src_dram, cidx, dst, eng2):
                        x = xraw.next()
                        k.dma(x[:], src_dram)
                        a = acc.next()
                        k.I(eng2, "tensor_scalar", W(a[:]), R(x[:]), R(convw[:, cidx, 2:3]), None, ALU.mult)
                        for (sa, sb_) in ((0, CTX), (CTX, TS)):
                            for j in (0, 1, 3, 4):
                                sft = j - 2
                                t0, t1 = max(sa, sa - sft), min(sb_, sb_ - sft)
                                if eng2 == "dve":
                                    k.I("dve", "scalar_tensor_tensor", W(a[:, t0:t1]), R(x[:, t0 + sft:t1 + sft]),
                                        R(convw[:, cidx, j:j + 1]), R(a[:, t0:t1]), ALU.mult, ALU.add)
                                else:
                                    tp = ctmp.next()
                                    k.I("pool", "tensor_scalar", W(tp[:, t0:t1]), R(x[:, t0 + sft:t1 + sft]),
                                        R(convw[:, cidx, j:j + 1]), None, ALU.mult)
                                    k.I("pool", "tensor_tensor", W(a[:, t0:t1]), R(a[:, t0:t1]), R(tp[:, t0:t1]), ALU.add)
                        k.I("act", "activation", W(dst[:]), R(a[:]), AF.Silu)

                    def l2n(t, scale):
                        for (b0, n) in blocks:
                            cs = slice(b0, b0 + n)
                            sq = sqr.next()
                            k.I("act", "activation", W(sq[:, 0:n]), R(t[:, cs]), AF.Square)
                            p1 = pss.next()
                            k.I("pe", "matmul", W(p1[:, 0:n]), R(C.ones[:]), R(sq[:, 0:n]), start=True, stop=True)
                            rs = rsr.next()
                            k.I("act", "activation", W(rs[:, 0:n]), R(p1[:, 0:n]), AF.Sqrt, bias=R(C.epsc[:]), scale=1.0)
                            k.I("dve", "reciprocal", W(rs[:, 0:n]), R(rs[:, 0:n]))
                            k.I("dve", "scalar_tensor_tensor", W(t[:, cs]), R(t[:, cs]), scale, R(rs[:, 0:n]), ALU.mult, ALU.mult)

                    for h in (2 * hp, 2 * hp + 1):
                        conv_chunk(C.pbq[h][:, tok], h, qn[h % 2], "dve")
                        l2n(qn[h % 2], QS)
                        conv_chunk(C.pbq[4 + h][:, tok], 4 + h, kn[h % 2], "pool")
                        l2n(kn[h % 2], 1.0)
                        conv_chunk(C.pbq[8 + h][:, tok], 8 + h, vs[h % 2], "dve")
                    k.end_stage()
                with ExitStack() as st2:
                    sb2, ps2, sbr2, psr2 = _alloc(nc, st2)
                    sc = load_scan_consts(C, k, sb2)
                    gn = sb2("gbn", [128, 1])
                    k.dma(gn[:], C.b_normT[li])
                    oacc = [sb2("doacc%d" % h, [128, TS]) for h in range(2)]
                    S = {(d, h): [sb2("dS%d_%d_%d" % (d, h, i), [128, 128]) for i in range(2)]
                         for d in range(2) for h in (2 * hp, 2 * hp + 1)}
                    sidx = {kk: 0 for kk in S}
                    banks = [st2.enter_context(nc.psum_tensor(_uname("dps"), [128, 4, 128], F32)) for _ in range(6)]
                    k.excl.update(b.name for b in banks)
                    pss2 = psr2("dpss2", 2, [128, 512])
                    names = ("gb", "ngb", "e1", "e2", "e3", "qe", "r2", "r3", "Lm", "LT", "AT", "X0", "X1",
                             "P0", "P1", "PT0", "PT1", "kbg", "kdec", "vb", "w", "u", "vn")
                    sets = [{n_: sb2("du%d%s" % (i, n_), [128, 128]) for n_ in names} for i in range(NSET)]
                    cols = [sb2("ducol%d" % i, [128, 8]) for i in range(NSET)]
                    for h in range(2):
                        k.I("pool", "memset", W(oacc[h][:]), 0.0)
                    for kk in S:
                        k.I("pool", "memset", W(S[kk][0][:]), 0.0)
                    done = {}

                    def ready(spec):
                        n, d, h = spec
                        return n == 0 or done.get((d, h), -1) >= n - 1

                    def mark_done(spec):
                        n, d, h = spec
                        done[(d, h)] = n

                    def unit(spec, si):
                        n, d, h = spec
                        T_ = sets[si]
                        col = cols[si]
                        bk = banks[si]
                        A0, A1, D0, D1 = (bk, 0), (bk, 1), (bk, 2), (bk, 3)

                        def Wp(sl, cs=None):
                            return W(sl[0][:, sl[1], :] if cs is None else sl[0][:, sl[1], cs], sl[1])

                        def Rp(sl, cs=None, rows=None):
                            if rows is not None:
                                return R(sl[0][rows, sl[1], :], sl[1])
                            return R(sl[0][:, sl[1], :] if cs is None else sl[0][:, sl[1], cs], sl[1])
                        p = orders[d][n]
                        pc = slice(p * 128, (p + 1) * 128)
                        triA, suf = sc["triA%d" % d], sc["suf%d" % d]
                        dh = d * 4 + h
                        gcol = gg_[:, p, dh:dh + 1]
                        bcol = beta[:, p, dh:dh + 1]
                        qn_, kn_, vs_ = qn[h % 2], kn[h % 2], vs[h % 2]
                        gb, ngb, e1, e2, e3, qe = T_["gb"], T_["ngb"], T_["e1"], T_["e2"], T_["e3"], T_["qe"]
                        k.I("pool", "tensor_scalar", W(gb[:]), R(C.ones[:]), R(gcol), None, ALU.mult)
                        k.I("pool", "tensor_scalar", W(ngb[:]), R(gb[:]), -1.0, None, ALU.mult)
                        k.I("pe", "matmul", Wp(A0), R(gb[:]), R(triA[:]), start=True, stop=True)
                        k.I("pe", "matmul", Wp(A1, slice(0, 64)), R(triA[:]), R(gb[:, 0:64]), start=True, stop=True)
                        k.I("pe", "matmul", Wp(A1, slice(64, 128)), R(suf[:]), R(gb[:, 0:64]), start=True, stop=True)
                        k.I("pe", "matmul", Wp(D0), R(triA[:]), R(gb[:]), start=True, stop=False)
                        k.I("pe", "matmul", Wp(D0), R(ngb[:]), R(triA[:]), start=False, stop=True)
                        k.I("pe", "matmul", Wp(D1), R(gb[:]), R(triA[:]), start=True, stop=False)
                        k.I("pe", "matmul", Wp(D1), R(triA[:]), R(ngb[:]), start=False, stop=True)
                        yield
                        r2, r3 = T_["r2"], T_["r3"]
                        k.I("act", "activation", W(e1[:]), Rp(A0), AF.Exp)
                        k.I("act", "activation", W(col[:, 0:1]), Rp(A1, slice(0, 1)), AF.Exp)
                        k.I("act", "activation", W(col[:, 2:3]), Rp(A1, slice(64, 65)), AF.Exp)
                        if _os.environ.get("RELU", "act") == "act":
                            k.I("act", "activation", W(r2[:]), Rp(D0), AF.Relu, scale=-1.0)
                            k.I("act", "activation", W(r3[:]), Rp(D1), AF.Relu, scale=-1.0)
                        else:
                            k.I("dve", "tensor_scalar", W(r2[:]), Rp(D0), -1.0, 0.0, ALU.mult, ALU.max)
                            k.I("dve", "tensor_scalar", W(r3[:]), Rp(D1), -1.0, 0.0, ALU.mult, ALU.max)
                        k.I("act", "activation", W(e2[:]), R(r2[:]), AF.Exp, scale=-1.0)
                        k.I("act", "activation", W(e3[:]), R(r3[:]), AF.Exp, scale=-1.0)
                        k.I("pe", "matmul", Wp(D0), R(kn_[:, pc]), R(kn_[:, pc]), start=True, stop=True)
                        k.I("pe", "matmul", Wp(D1), R(kn_[:, pc]), R(qn_[:, pc]), start=True, stop=True)
                        yield
                        k.I("pool", "tensor_tensor", W(col[:, 4:5]), R(col[:, 0:1]), R(bcol), ALU.mult)
                        k.I("pool", "tensor_tensor", W(e2[:]), R(e2[:]), R(suf[:]), ALU.mult)
                        k.I("pool", "tensor_tensor", W(e3[:]), R(e3[:]), R(triA[:]), ALU.mult)
                        k.I("pool", "tensor_tensor", W(qe[:]), R(qn_[:, pc]), R(e1[:]), ALU.mult)
                        yield
                        Lm, LT, AT = T_["Lm"], T_["LT"], T_["AT"]
                        k.I("dve", "scalar_tensor_tensor", W(Lm[:]), Rp(D0), R(bcol), R(e2[:]), ALU.mult, ALU.mult)
                        k.I("dve", "tensor_tensor", W(AT[:]), Rp(D1), R(e3[:]), ALU.mult)
                        k.I("pe", "transpose", Wp(A0), R(Lm[:]), R(C.ident[:]))
                        yield
                        X = T_["X0"]
                        k.I("act", "copy", W(LT[:]), Rp(A0))
                        kbg, kdec, vb = T_["kbg"], T_["kdec"], T_["vb"]
                        yield
                        k.I("dve", "scalar_tensor_tensor", W(X[:]), R(LT[:]), -1.0, R(C.ident[:]), ALU.mult, ALU.add)
                        Pc, PTc = Lm, LT
                        k.I("pe", "matmul", Wp(A1), R(PTc[:]), R(Pc[:]), start=True, stop=True)
                        k.I("pe", "matmul", Wp(D0), R(Pc[:]), R(PTc[:]), start=True, stop=True)
                        yield
                        for it in range(5):
                            Pn, PTn = T_["P%d" % (it % 2)], T_["PT%d" % (it % 2)]
                            k.I("act", "copy", W(Pn[:]), Rp(A1))
                            if it < 4:
                                if it % 2 == 0:
                                    k.I("act", "copy", W(PTn[:]), Rp(D0))
                                else:
                                    k.I("dve", "tensor_copy", W(PTn[:]), Rp(D0))
                            yield
                            k.I("pe", "matmul", Wp(D1), R(Pn[:]), R(X[:]), start=True, stop=True)
                            if it < 4:
                                k.I("pe", "matmul", Wp(A1), R(PTn[:]), R(Pn[:]), start=True, stop=True)
                                if it < 3:
                                    k.I("pe", "matmul", Wp(D0), R(Pn[:]), R(PTn[:]), start=True, stop=True)
                            yield
                            Xn = T_["X%d" % ((it + 1) % 2)]
                            k.I("dve", "tensor_tensor", W(Xn[:]), R(X[:]), Rp(D1), ALU.add)
                            X, Pc, PTc = Xn, Pn, PTn
                        yield
                        k.I("pe", "transpose", Wp(D0), R(kn_[:, pc]), R(C.ident[:]))
                        k.I("pe", "transpose", Wp(D1), R(vs_[:, pc]), R(C.ident[:]))
                        yield
                        k.I("dve", "tensor_scalar", W(kbg[:]), Rp(D0), R(col[:, 4:5]), None, ALU.mult)
                        k.I("dve", "tensor_scalar", W(kdec[:]), Rp(D0), R(col[:, 2:3]), None, ALU.mult)
                        k.I("dve", "tensor_scalar", W(vb[:]), Rp(D1), R(bcol), None, ALU.mult)
                        yield
                        w_, u_ = T_["w"], T_["u"]
                        k.I("pe", "matmul", Wp(A0), R(kbg[:]), R(X[:]), start=True, stop=True)
                        k.I("pe", "matmul", Wp(A1), R(X[:]), R(vb[:]), start=True, stop=True)
                        yield
                        k.I("act", "copy", W(w_[:]), Rp(A0))
                        k.I("act", "copy", W(u_[:]), Rp(A1))
                        yield "S"
                        vn = T_["vn"]
                        Sl = S[(d, h)]
                        for c in ((0, 1) if d == 0 else (1, 0)):
                            cs = slice(c * 64, (c + 1) * 64)
                            Sc = Sl[sidx[(d, h)] % 2]
                            Sn = Sl[(sidx[(d, h)] + 1) % 2]
                            sidx[(d, h)] += 1
                            k.I("pe", "matmul", Wp(D0), R(w_[:]), R(Sc[:]), start=True, stop=True)
                            yield
                            k.I("dve", "scalar_tensor_tensor", W(vn[cs, :], c), Rp(D0, rows=cs), -1.0, R(u_[cs, :]),
                                ALU.mult, ALU.add)
                            yield
                            k.I("pe", "matmul", Wp(D1, cs), R(Sc[:]), R(qe[:, cs]), start=True, stop=False)
                            k.I("pe", "matmul", Wp(D1, cs), R(vn[cs, :], c), R(AT[cs, cs]), start=False, stop=True)
                            k.I("pe", "matmul", Wp(D0), R(kdec[cs, :]), R(vn[cs, :], c), start=True, stop=True)
                            yield
                            lc = c * 64 + (63 if d == 0 else 0)
                            k.I("dve", "scalar_tensor_tensor", W(Sn[:]), R(Sc[:]), R(e1[:, lc:lc + 1]), Rp(D0),
                                ALU.mult, ALU.add)
                        k.I("dve", "tensor_tensor", W(oacc[h % 2][:, pc], p), R(oacc[h % 2][:, pc], p), Rp(D1), ALU.add)

                    specs = [(n, d, h) for n in range(TPS) for d in range(2) for h in (2 * hp, 2 * hp + 1)]
                    run_units(specs, unit, NSET, ready, mark_done)
                    tmp = (sbr2("dsq", 3, [128, 512]), pss2, sbr2("drs", 3, [128, 512]), sbr2("don", 3, [128, 512]),
                           sbr2("dgg", 3, [128, 512], BF16), sbr2("dob", 2, [128, TS], BF16))
                    for h in (2 * hp, 2 * hp + 1):
                        head_norm_out(C, k, sb2, oacc[h % 2], gn[:, 0:1], C.pbz[h], C.bout[h], tok0, tmp)
                    k.end_stage()


def stage_merge(C, li):
    k, nc = C.k, C.nc
    wbr_d = C.w_branch[li]
    wout_d = C.w_out[li]
    with ExitStack() as st:
        sb, ps, sbr, psr = _alloc(nc, st)
        WA = sb("mWA", [64, 8, D], BF16)
        WB = sb("mWB", [128, 4, D], BF16)
        WC = sb("mWC", [128, 4, D], BF16)
        WO = sb("mWO", [128, 8, D], BF16)
        stg = sbr("mstg", 2, [128, D])
        g5 = [sb("mg5%d" % r, [128, D]) for r in range(3)]
        for r in range(3):
            k.dma(g5[r][:], C.mod_dram[li][r:r + 1, 5 * D:6 * D].partition_broadcast(128))
        i = 0
        for h in range(8):
            w = stg.next()
            k.dma(w[0:64, :], wbr_d[h * 64:(h + 1) * 64, :])
            k.I("pool" if i % 2 else "dve", "tensor_copy", W(WA[:, h, :], h), R(w[0:64, :]))
            i += 1
        for (Wt, base) in ((WB, 512), (WC, 1024)):
            for c in range(4):
                w = stg.next()
                k.dma(w[:], wbr_d[base + c * 128:base + (c + 1) * 128, :])
                k.I("pool" if i % 2 else "dve", "tensor_copy", W(Wt[:, c, :], c), R(w[:]))
                i += 1
        for c in range(8):
            w = stg.next()
            k.dma(w[:], wout_d[c * 128:(c + 1) * 128, :])
            k.I("pool" if i % 2 else "dve", "tensor_copy", W(WO[:, c, :], c), R(w[:]))
            i += 1
        aor = sbr("mao", 2, [64, 8, 512], BF16)
        bor = sbr("mbo", 2, [128, 4, 512], BF16)
        cor = sbr("mco", 2, [128, 4, 512], BF16)
        gtr = sbr("mgt", 2, [128, 24, 512], BF16)
        mTr = sbr("mmT", 2, [128, 8, 512], BF16)
        pA = psr("mpA", 2, [128, 512])
        pB = psr("mpB", 2, [128, 512])
        pCc = psr("mpC", 2, [128, 512])
        pY = psr("mpY", 2, [128, 512])
        t1r = sbr("mt1", 2, [128, 512])
        t2r = sbr("mt2", 2, [128, 512])
        t3r = sbr("mt3", 2, [128, 512])
        xt = sbr("mx", 2, [128, D])
        tmr = sbr("mtm", 2, [128, D])
        xor_ = sbr("mxo", 2, [128, D])
        for tb in range(NT // 512):
            ts = slice(tb * 512, (tb + 1) * 512)
            ao, bo, co, gt = aor.next(), bor.next(), cor.next(), gtr.next()
            k.dma(ao[:], C.aout[:, :, ts].rearrange("h p n -> p h n"))
            k.dma(bo[:], C.bout[:, :, ts].rearrange("h p n -> p h n"))
            k.dma(co[:], C.cout[:, :, ts].rearrange("h p n -> p h n"))
            k.dma(gt[:], C.pgate[:, :, ts].rearrange("h p n -> p h n"))
            mT = mTr.next()
            for dc in range(8):
                ds_ = slice(dc * 128, (dc + 1) * 128)
                a_ps, b_ps, c_ps = pA.next(), pB.next(), pCc.next()
                for h in range(8):
                    k.I("pe", "matmul", W(a_ps[:]), R(WA[:, h, ds_]), R(ao[:, h, :]), start=(h == 0), stop=(h == 7))
                for c in range(4):
                    k.I("pe", "matmul", W(b_ps[:]), R(WB[:, c, ds_]), R(bo[:, c, :]), start=(c == 0), stop=(c == 3))
                for c in range(4):
                    k.I("pe", "matmul", W(c_ps[:]), R(WC[:, c, ds_]), R(co[:, c, :]), start=(c == 0), stop=(c == 3))
                t1, t2, t3 = t1r.next(), t2r.next(), t3r.next()
                k.I("dve", "tensor_tensor", W(t1[:]), R(a_ps[:]), R(gt[:, dc, :]), ALU.mult)
                k.I("dve", "tensor_tensor", W(t2[:]), R(b_ps[:]), R(gt[:, 8 + dc, :]), ALU.mult)
                k.I("dve", "tensor_tensor", W(t3[:]), R(c_ps[:]), R(gt[:, 16 + dc, :]), ALU.mult)
                k.I("pool", "tensor_tensor", W(t1[:]), R(t1[:]), R(t2[:]), ALU.add)
                k.I("pool", "tensor_tensor", W(mT[:, dc, :], dc), R(t1[:]), R(t3[:]), ALU.add)
            for q in range(4):
                t = tb * 4 + q
                r = mod_row(t)
                x = xt.next()
                k.dma(x[:], C.xres[t * 128:(t + 1) * 128, :], ikey=t)
                tm = tmr.next()
                for half in range(2):
                    p = pY.next()
                    hs = slice(half * 512, (half + 1) * 512)
                    for dc in range(8):
                        k.I("pe", "matmul", W(p[:]), R(mT[:, dc, q * 128:(q + 1) * 128]), R(WO[:, dc, hs]),
                            start=(dc == 0), stop=(dc == 7))
                    k.I("dve", "tensor_tensor", W(tm[:, hs], half), R(p[:]), R(g5[r][:, hs]), ALU.mult)
                xo = xor_.next()
                k.I("pool", "tensor_tensor", W(xo[:]), R(x[:]), R(tm[:]), ALU.add)
                k.dma(C.xres[t * 128:(t + 1) * 128, :], xo[:], eng="act", okey=t)
        k.end_stage()


def host_consts():
    c = {}
    c["c_ident"] = np.eye(128, dtype=np.float32)
    perm = np.zeros((64, 64), np.float32)
    for m in range(64):
        perm[(m + 32) % 64, m] = 1.0
    c["c_perm64"] = perm
    pos = np.arange(SEQ)
    row_pos = (pos // 64).astype(np.float32)
    col_pos = (pos % 64).astype(np.float32)
    inv = (10000.0 ** (-np.arange(0, 32, 2, dtype=np.float32) / 32.0)).astype(np.float32)
    ang = np.concatenate([row_pos[:, None] * inv, col_pos[:, None] * inv], axis=-1)
    cos, sin = np.cos(ang).astype(np.float32), np.sin(ang).astype(np.float32)
    cosF = np.ones((64, TS), np.float32)
    sinS = np.zeros((64, TS), np.float32)
    cosF[0:32, CTX:] = cos.T
    cosF[32:64, CTX:] = cos.T
    sinS[0:32, CTX:] = -sin.T
    sinS[32:64, CTX:] = sin.T
    c["c_cosF"], c["c_sinS"] = cosF, sinS
    cc, rr = np.meshgrid(np.arange(128), np.arange(128), indexing="ij")
    c["c_maskLo4"] = np.tile((cc >= rr).astype(np.float32), (1, 4))
    c["c_maskHi4"] = np.tile((cc <= rr).astype(np.float32), (1, 4))
    j = np.arange(128)[:, None]
    i = np.arange(128)[None, :]
    same = (j // 64) == (i // 64)
    jl = j % 64
    f32 = np.float32
    triA0 = (same & (j <= i)).astype(f32)
    triA1 = (same & (j >= i)).astype(f32)
    c["c_triA0"], c["c_triA1"] = triA0, triA1
    c["c_triM0"] = triA0 - (same & (jl <= 31)).astype(f32)
    c["c_triM1"] = triA1 - (same & (jl >= 32)).astype(f32)
    c["c_suf0"] = (same & (j > i)).astype(f32)
    c["c_suf1"] = (same & (j < i)).astype(f32)
    return c


CONST_SHAPES = {"c_ident": [128, 128], "c_perm64": [64, 64], "c_cosF": [64, TS], "c_sinS": [64, TS],
                "c_maskLo4": [128, 512], "c_maskHi4": [128, 512],
                "c_triA0": [128, 128], "c_triA1": [128, 128], "c_triM0": [128, 128], "c_triM1": [128, 128],
                "c_suf0": [128, 128], "c_suf1": [128, 128]}


def build_program(nlayers, last_flags, upto="all", dbg=(), delta_stop=0, want_ctx_out=False):
    nc = bass.Bass("TRN2", target_bir_lowering=False)
    C = Ctx()
    C.delta_stop = delta_stop
    import os as _os
    C.unit_pool = _os.environ.get("UNITPOOL", "pool")
    C.one_unit = bool(int(_os.environ.get("ONEUNIT", "0")))
    C.nc = nc
    C.dbg = dbg

    def din(name, shape, dt=F32):
        return nc.dram_tensor(name, list(shape), dt, kind="ExternalInput").ap()

    def dscr(name, shape, dt=F32):
        kind = "ExternalOutput" if name in dbg else "Internal"
        if upto == "delta_only" and name in ("pbq", "tmba", "pbz"):
            kind = "ExternalInput"
        return nc.dram_tensor(name, list(shape), dt, kind=kind).ap()

    L = nlayers
    C.L = L
    C._din = din
    C._used = []
    C._lazy = {
        "x_in": [NSEQ, SEQ, D], "ctx_in": [NSEQ, CTX, D], "cvecT": [128, 8, 3], "c_lb": [1, DEPTH, 1024],
        "w_ada": [L, D, 9 * D], "b_ada": [L, 9 * D], "norm_g": [L, 3 * D],
        "w_ffn_gu": [L, 2, D, 2 * DFF], "w_ffn_d": [L, 2, DFF, D], "w_in": [L, D, IN_W],
        "qknT": [L, 64, 2], "a_sink": [L, 8], "c_normT": [L, 128, 1], "b_normT": [L, 128, 1],
        "convT": [L, 128, 12, 5], "b_a_log": [L, 8], "b_dt_bias": [L, 8],
        "w_branch": [L, 1536, D], "w_out": [L, D, D], "lbmask": [1, L * DEPTH],
    }
    for n_, shp_ in CONST_SHAPES.items():
        C._lazy[n_] = shp_

    class _Cst:
        def __getitem__(self, nm):
            return getattr(C, "c_" + nm)
    C.cst = _Cst()
    C.y = nc.dram_tensor("y", [NSEQ, SEQ, D], F32, kind="ExternalOutput").ap()
    C.yc = nc.dram_tensor("yc", [NSEQ, CTX, D], F32, kind="ExternalOutput").ap() if want_ctx_out else None

    C.xres = dscr("xres", [NT, D])
    C.mod_dram = [dscr("mod%d" % i, [3, 9 * D]) for i in range(L)]
    C.lb_dram = dscr("lb_dram", [1, L, 1024])
    C.actT = dscr("actT", [NF, 128, NT], BF16)
    C.pq = dscr("pq", [8, 64, NT])
    C.pk = dscr("pk", [2, 64, NT])
    C.pbq = dscr("pbq", [12, 128, NT])
    C.pbz = dscr("pbz", [4, 128, NT], BF16)
    C.pcq = dscr("pcq", [4, 128, NT])
    C.pcg = dscr("pcg", [4, 128, NT], BF16)
    C.pgate = dscr("pgate", [24, 128, NT], BF16)
    C.tmv = dscr("tmv", [NT, 128])
    C.tmba = dscr("tmba", [NT, 16])
    C.tmf = dscr("tmf", [NT, 1024])
    C.tmi = dscr("tmi", [NT, 512])
    C.aout = dscr("aout", [8, 64, NT], BF16)
    C.bout = dscr("bout", [4, 128, NT], BF16)
    C.cout = dscr("cout", [4, 128, NT], BF16)

    with ExitStack() as st:
        C.k = k = K(nc, st, nepochs=L)
        sb, ps, sbr, psr = _alloc(nc, st)
        C.scT = sb("scT", [128, 8, 3])
        C.ident = sb("ident", [128, 128])
        C.identb = sb("identb", [128, 128], BF16)
        C.ones = sb("ones", [128, 128])
        C.epsc = sb("epsc", [128, 1])
        k.I("dve", "memset", W(C.epsc[:]), EPS)
        if upto != "delta_only":
            stage_pre(C)

        def with_hT(fn):
            with nc.sbuf_tensor(_uname("hT"), [128, 8, NT], BF16) as hT:
                C.hT = hT
                fn()
            C.hT = None

        def run_layers():
            for li in range(L):
                if li > 0:
                    k.new_epoch()
                if upto == "delta_only":
                    k.I("pool", "memset", W(C.ones[:]), 1.0)
                    k.dma(C.ident[:], C.cst["ident"])
                    stage_delta(C, li)
                    return
                stage_mod(C, li)
                with_hT(lambda: (stage_norm(C, li, 0), stage_ffn_up(C, C.w_ffn_gu[li, 0])))
                stage_ffn_down(C, li, C.w_ffn_d[li, 0], 2)
                if upto == "ffn0":
                    return
                with_hT(lambda: (stage_norm(C, li, 1), stage_in_proj(C, li)))
                stage_attn(C, li)
                if upto == "attn":
                    return
                stage_gla(C, li)
                if upto == "gla":
                    return
                stage_delta(C, li)
                if upto == "mixers":
                    return
                stage_merge(C, li)
                if upto == "mix":
                    return
                with_hT(lambda: (stage_norm(C, li, 2), stage_ffn_up(C, C.w_ffn_gu[li, 1])))
                stage_ffn_down(C, li, C.w_ffn_d[li, 1], 8)
        run_layers()
        for s in range(NSEQ):
            k.dma(C.y[s], C.xres[s * TS + CTX:(s + 1) * TS, :])
            if C.yc is not None:
                k.dma(C.yc[s], C.xres[s * TS:s * TS + CTX, :])
        k.end_stage()
        print("ops", k.n_ops, "waits", k.n_waits)
    nc._used_inputs = list(C._used)
    return nc


def make_in_maps(inputs, layers, x=None, ctx=None):
    f = lambda a: np.ascontiguousarray(a, dtype=np.float32)
    c, c_ctx = inputs["c"], inputs["c_ctx"]
    x = inputs["x"] if x is None else x
    ctx = inputs["ctx"] if ctx is None else ctx
    nl = len(layers)
    shared = {
        "c_lb": f(inputs["c_lb"]).reshape(1, DEPTH, 1024),
        "w_ada": f(inputs["w_ada"][layers]),
        "b_ada": f(inputs["b_ada"][layers]),
        "norm_g": f(inputs["norm_g"][layers]).reshape(nl, 3 * D),
        "w_ffn_gu": f(inputs["w_ffn_gu"][layers]),
        "w_ffn_d": f(inputs["w_ffn_d"][layers]),
        "w_in": f(inputs["w_in"][layers]),
        "qknT": f(np.transpose(inputs["a_qk_norm"][layers], (0, 2, 1))),
        "a_sink": f(inputs["a_sink"][layers]),
        "w_branch": f(inputs["w_branch"][layers]),
        "w_out": f(inputs["w_out"][layers]),
        "lbmask": f(np.array([[1.0 if 1 <= j <= l else 0.0 for j in range(DEPTH)] for l in layers])).reshape(1, nl * DEPTH),
        "c_normT": f(inputs["c_norm"][layers]).reshape(nl, 128, 1),
        "b_normT": f(inputs["b_norm"][layers]).reshape(nl, 128, 1),
        "convT": f(np.transpose(inputs["b_conv"][layers].reshape(nl, 5, 12, 128), (0, 3, 2, 1))),
        "b_a_log": f(inputs["b_a_log"][layers]).reshape(nl, 8),
        "b_dt_bias": f(inputs["b_dt_bias"][layers]).reshape(nl, 8),
    }
    shared.update(host_consts())
    maps = []
    for core in range(NCORE):
        b0 = core * NSEQ
        cv = np.stack([c[b0], c[b0 + 1], c_ctx], axis=0)
        cvT = f(cv.reshape(3, 8, 128).transpose(2, 1, 0))
        m = dict(shared)
        m["x_in"] = f(x[b0:b0 + NSEQ])
        m["ctx_in"] = f(ctx[b0:b0 + NSEQ])
        m["cvecT"] = cvT
        maps.append(m)
    return maps


_PROG = {}


def _get_prog(key, **kw):
    if key not in _PROG:
        _PROG[key] = build_program(**kw)
    return _PROG[key]


def kernel(**inputs):
    inputs = {k_: np.asarray(v) for k_, v in inputs.items()}
    if _os.environ.get("KERNEL_MODE", "fused") == "fused":
        nc = _get_prog("l4", nlayers=DEPTH, last_flags=[False] * DEPTH, want_ctx_out=False)
        maps = make_in_maps(inputs, list(range(DEPTH)))
        maps = [{k_: v for k_, v in m.items() if k_ in nc._used_inputs} for m in maps]
        res = run_bass_kernel_spmd(nc, maps, core_ids=list(range(NCORE)))
        return np.concatenate([np.asarray(r["y"]) for r in res.results], axis=0).astype(np.float32)
    nc = _get_prog("l1", nlayers=1, last_flags=[False], want_ctx_out=True)
    x, ctx = inputs["x"], inputs["ctx"]
    for li in range(DEPTH):
        maps = make_in_maps(inputs, [li], x=x, ctx=ctx)
        maps = [{k_: v for k_, v in m.items() if k_ in nc._used_inputs} for m in maps]
        res = run_bass_kernel_spmd(nc, maps, core_ids=list(range(NCORE)))
        x = np.concatenate([np.asarray(r["y"]) for r in res.results], axis=0)
        ctx = np.concatenate([np.asarray(r["yc"]) for r in res.results], axis=0)
    return x.astype(np.float32)
```
